# Optimizing a Trainium2 kernel written in Bass

```python
import math
import jax, jax.numpy as jnp
from jax import lax
import numpy as np

D_MODEL = 2048
BATCH = 4
SEQ = 4096
DEPTH = 1

S5_WIDTH = D_MODEL // 2
S5_GROUP = 16
S5_GROUPS = S5_WIDTH // S5_GROUP
S5_STATE = 64
DT_MIN = 1e-3
DT_MAX = 1e-1
S5_MAX_RE = -1e-4
HGRN_WIDTH = D_MODEL // 2
HGRN_EXPAND = 128
HGRN_HEADS = HGRN_WIDTH // HGRN_EXPAND
HGRN_HEAD_DIM = HGRN_EXPAND
HGRN_CHUNK = 64
D_FF = 5632
CONV_WIDTH = 3
RMS_EPS = 1e-6
N_IN = S5_WIDTH + 4 * HGRN_WIDTH + 2 * D_MODEL

kernel_name = 'hybrid_s5_hgrn2_gated_merge_block'


def _rmsnorm(x, g):
    xf = x.astype(jnp.float32)
    y = xf * lax.rsqrt(jnp.mean(xf * xf, axis=-1, keepdims=True) + RMS_EPS)
    return (y * g.astype(jnp.float32)).astype(x.dtype)


def _in_splits():
    sizes = [S5_WIDTH, HGRN_WIDTH, HGRN_WIDTH, HGRN_WIDTH, HGRN_WIDTH, D_MODEL, D_MODEL]
    offs, acc = [], 0
    for s in sizes[:-1]:
        acc += s
        offs.append(acc)
    return offs


def _complex_affine_combine(first, second):
    a1r, a1i, b1r, b1i = first
    a2r, a2i, b2r, b2i = second
    return (a2r * a1r - a2i * a1i,
            a2r * a1i + a2i * a1r,
            a2r * b1r - a2i * b1i + b2r,
            a2r * b1i + a2i * b1r + b2i)


def _s5_branch(u, a_re, a_im, log_dt, b_re, b_im, c_re, c_im, d, w_glu, b_glu):
    bsz, L, _ = u.shape
    f32 = jnp.float32
    uf = u.astype(f32).reshape(bsz, L, S5_GROUPS, S5_GROUP)
    lam_re = jnp.minimum(a_re.astype(f32), S5_MAX_RE)
    lam_im = a_im.astype(f32)
    dt = jnp.exp(log_dt.astype(f32))[:, None]
    mag = jnp.exp(lam_re * dt)
    abar_re = mag * jnp.cos(lam_im * dt)
    abar_im = mag * jnp.sin(lam_im * dt)
    den = lam_re * lam_re + lam_im * lam_im
    nr = abar_re - 1.0
    ni = abar_im
    coef_re = (nr * lam_re + ni * lam_im) / den
    coef_im = (ni * lam_re - nr * lam_im) / den
    bu_re = jnp.einsum('blgc,gpc->blgp', uf, b_re.astype(f32))
    bu_im = jnp.einsum('blgc,gpc->blgp', uf, b_im.astype(f32))
    bb_re = coef_re * bu_re - coef_im * bu_im
    bb_im = coef_re * bu_im + coef_im * bu_re
    a_seq_re = jnp.broadcast_to(abar_re, (1, L, S5_GROUPS, S5_STATE))
    a_seq_im = jnp.broadcast_to(abar_im, (1, L, S5_GROUPS, S5_STATE))
    _, _, s_re, s_im = lax.associative_scan(
        _complex_affine_combine, (a_seq_re, a_seq_im, bb_re, bb_im), axis=1)
    y = (jnp.einsum('blgp,gcp->blgc', s_re, c_re.astype(f32))
         - jnp.einsum('blgp,gcp->blgc', s_im, c_im.astype(f32))
         + d.astype(f32) * uf)
    y = y.reshape(bsz, L, S5_WIDTH)
    z = jax.nn.gelu(y)
    z = z * jax.nn.sigmoid(z @ w_glu.astype(f32) + b_glu.astype(f32))
    return z.astype(u.dtype)


def _hgrn2_branch(q_in, f_in, i_in, g_in, lb, norm_g):
    bsz, L, _ = q_in.shape
    f32 = jnp.float32
    n_chunks = L // HGRN_CHUNK

    def heads(t):
        return t.reshape(bsz, n_chunks, HGRN_CHUNK, HGRN_HEADS, HGRN_HEAD_DIM).transpose(1, 0, 3, 2, 4)

    q = jax.nn.silu(q_in.astype(f32))
    zf = f_in.astype(f32)
    lbf = lb.astype(f32)
    log_f = jnp.logaddexp(jnp.log(lbf), jnp.log1p(-lbf) + jax.nn.log_sigmoid(zf))
    k = (1.0 - lbf) * jax.nn.sigmoid(-zf)
    v = i_in.astype(f32)
    causal = jnp.tril(jnp.ones((HGRN_CHUNK, HGRN_CHUNK), dtype=bool))

    def step(S, blk):
        qc, lfc, kc, vc = blk
        b = jnp.cumsum(lfc, axis=2)
        b_last = b[:, :, -1:, :]
        inter = jnp.einsum('bhtk,bhkv->bhtv', qc * jnp.exp(b), S)
        diff = b[:, :, :, None, :] - b[:, :, None, :, :]
        decay = jnp.exp(jnp.where(causal[None, None, :, :, None], diff, -jnp.inf))
        scores = jnp.einsum('bhtk,bhsk,bhtsk->bhts', qc, kc, decay)
        intra = jnp.einsum('bhts,bhsv->bhtv', scores, vc)
        S_new = (jnp.exp(b_last[:, :, 0, :])[..., None] * S
                 + jnp.einsum('bhsk,bhsv->bhkv', kc * jnp.exp(b_last - b), vc))
        return S_new, inter + intra

    S0 = jnp.zeros((bsz, HGRN_HEADS, HGRN_HEAD_DIM, HGRN_HEAD_DIM), f32)
    _, o = lax.scan(step, S0, (heads(q), heads(log_f), heads(k), heads(v)))
    o = o.transpose(1, 0, 3, 2, 4).reshape(bsz, L, HGRN_HEADS, HGRN_HEAD_DIM)
    o = o * lax.rsqrt(jnp.mean(o * o, axis=-1, keepdims=True) + RMS_EPS)
    o = o * norm_g.astype(f32).reshape(HGRN_HEADS, HGRN_HEAD_DIM)
    o = o.reshape(bsz, L, HGRN_WIDTH) * jax.nn.silu(g_in.astype(f32))
    return o.astype(q_in.dtype)


def _conv_glu_ffn(h, w_up, conv_w, conv_b, w_down):
    L = h.shape[1]
    up = h @ w_up
    up_pad = jnp.pad(up, ((0, 0), (CONV_WIDTH - 1, 0), (0, 0)))
    conv = conv_b
    for j in range(CONV_WIDTH):
        conv = conv + conv_w[j] * up_pad[:, j:j + L, :]
    gate, val = jnp.split(conv, 2, axis=-1)
    return (jax.nn.silu(gate) * val) @ w_down


def setup_inputs(seed: int = 0) -> dict:
    key = jax.random.key(seed)
    ks = jax.random.split(key, 24)
    f32 = jnp.float32

    def nrm(k, shape, scale):
        return jax.random.normal(k, shape, f32) * scale

    G, P = S5_GROUPS, S5_STATE
    return {
        'x': nrm(ks[0], (BATCH, SEQ, D_MODEL), 1.0),
        'ln_mix_g': 1.0 + nrm(ks[1], (DEPTH, D_MODEL), 0.01),
        'w_in': nrm(ks[2], (DEPTH, D_MODEL, N_IN), D_MODEL ** -0.5),
        's5_a_re': -0.5 + nrm(ks[3], (DEPTH, G, P), 0.01),
        's5_a_im': math.pi * jnp.arange(P, dtype=f32) + nrm(ks[4], (DEPTH, G, P), 0.01),
        's5_log_dt': jax.random.uniform(ks[5], (DEPTH, G), f32, math.log(DT_MIN), math.log(DT_MAX)),
        's5_b_re': nrm(ks[6], (DEPTH, G, P, S5_GROUP), (2 * S5_GROUP) ** -0.5),
        's5_b_im': nrm(ks[7], (DEPTH, G, P, S5_GROUP), (2 * S5_GROUP) ** -0.5),
        's5_c_re': nrm(ks[8], (DEPTH, G, S5_GROUP, P), S5_STATE ** -0.5),
        's5_c_im': nrm(ks[9], (DEPTH, G, S5_GROUP, P), S5_STATE ** -0.5),
        's5_d': nrm(ks[10], (DEPTH, G, S5_GROUP), 1.0),
        's5_w_glu': nrm(ks[11], (DEPTH, S5_WIDTH, S5_WIDTH), S5_WIDTH ** -0.5),
        's5_b_glu': nrm(ks[12], (DEPTH, S5_WIDTH), 0.01),
        'w_proj_s5': nrm(ks[13], (DEPTH, S5_WIDTH, D_MODEL), S5_WIDTH ** -0.5),
        'hgrn_lb_logits': nrm(ks[14], (DEPTH + 1, HGRN_WIDTH), 0.1),
        'hgrn_norm_g': 1.0 + nrm(ks[15], (DEPTH, HGRN_WIDTH), 0.01),
        'w_proj_hgrn': nrm(ks[16], (DEPTH, HGRN_WIDTH, D_MODEL), HGRN_WIDTH ** -0.5),
        'w_out': nrm(ks[17], (DEPTH, D_MODEL, D_MODEL), D_MODEL ** -0.5),
        'ln_ffn_g': 1.0 + nrm(ks[18], (DEPTH, D_MODEL), 0.01),
        'w_up': nrm(ks[19], (DEPTH, D_MODEL, 2 * D_FF), D_MODEL ** -0.5),
        'conv_w': nrm(ks[20], (DEPTH, CONV_WIDTH, 2 * D_FF), CONV_WIDTH ** -0.5),
        'conv_b': nrm(ks[21], (DEPTH, 2 * D_FF), 0.01),
        'w_down': nrm(ks[22], (DEPTH, D_FF, D_MODEL), D_FF ** -0.5),
        'ln_final_g': 1.0 + nrm(ks[23], (D_MODEL,), 0.01),
    }


def reference(x, ln_mix_g, w_in, s5_a_re, s5_a_im, s5_log_dt, s5_b_re, s5_b_im, s5_c_re, s5_c_im,
              s5_d, s5_w_glu, s5_b_glu, w_proj_s5, hgrn_lb_logits, hgrn_norm_g, w_proj_hgrn,
              w_out, ln_ffn_g, w_up, conv_w, conv_b, w_down, ln_final_g):
    lb_all = jnp.cumsum(jax.nn.softmax(hgrn_lb_logits.astype(jnp.float32), axis=0), axis=0)
    for l in range(DEPTH):
        h = _rmsnorm(x, ln_mix_g[l])
        proj = h @ w_in[l]
        u_s5, q_h, f_h, i_h, g_h, gate_s5, gate_hgrn = jnp.split(proj, _in_splits(), axis=-1)
        y_s5 = _s5_branch(u_s5, s5_a_re[l], s5_a_im[l], s5_log_dt[l], s5_b_re[l], s5_b_im[l],
                          s5_c_re[l], s5_c_im[l], s5_d[l], s5_w_glu[l], s5_b_glu[l]) @ w_proj_s5[l]
        y_hgrn = _hgrn2_branch(q_h, f_h, i_h, g_h, lb_all[l], hgrn_norm_g[l]) @ w_proj_hgrn[l]
        merged = jax.nn.sigmoid(gate_s5) * y_s5 + jax.nn.sigmoid(gate_hgrn) * y_hgrn
        x = x + merged @ w_out[l]
        x = x + _conv_glu_ffn(_rmsnorm(x, ln_ffn_g[l]), w_up[l], conv_w[l], conv_b[l], w_down[l])
    return _rmsnorm(x, ln_final_g)
```

```python
import contextlib
import math
import numpy as np
import concourse.bass as bass
import concourse.mybir as mybir
from concourse.bass_utils import run_bass_kernel_spmd

F32 = mybir.dt.float32
BF16 = mybir.dt.bfloat16
I32 = mybir.dt.int32
AF = mybir.ActivationFunctionType
ALU = mybir.AluOpType

D = 2048
TT = 256
NSUB = TT // 128
NCH = TT // 64
NK8 = TT // 8
HALF = 2048
NWARM = HALF // TT
NTILE = 2 * NWARM
DFF = 5632
NFC = DFF // 128
EPS = 1e-6
NSLOT = 6
SLOT = 4096
NRING = 5
NTMP = 8


class Sem:
    def __init__(self, h):
        self.h = h
        self.cnt = 0


class Res:
    __slots__ = ("w", "r")

    def __init__(self):
        self.w = None
        self.r = {}


class Eng:
    def __init__(self, q, sem, is_pe=False):
        self.q = q
        self.sem = sem
        self.seen = {}
        self.is_pe = is_pe


class KB:
    def __init__(self, nc, es):
        self.nc = nc
        self.es = es
        self.nsem = 0
        self.PE = Eng(nc.tensor, self.sem("pe"), True)
        self.ACT = Eng(nc.scalar, self.sem("act"))
        self.DVE = Eng(nc.vector, self.sem("dve"))
        self.POOL = Eng(nc.gpsimd, self.sem("pool"))
        self.SP = Eng(nc.sync, self.sem("sp"))
        self.engs = [self.PE, self.ACT, self.DVE, self.POOL, self.SP]
        self.allsems = []

    def sem(self, name):
        self.nsem += 1
        s = Sem(self.es.enter_context(self.nc.semaphore(f"{name}_{self.nsem}")))
        return s

    def sb(self, name, shape, dt):
        return self.es.enter_context(self.nc.sbuf_tensor(name, shape, dt))

    def _waits(self, eng, R, W):
        deps = {}

        def add(tok):
            if tok is None:
                return
            s, v = tok
            if deps.get(s, 0) < v:
                deps[s] = v
        for r in R:
            add(r.w)
        for w in W:
            add(w.w)
            for s, v in w.r.items():
                add((s, v))
        for s, v in deps.items():
            if eng.is_pe and s is eng.sem:
                continue
            if eng.seen.get(s, 0) >= v:
                continue
            eng.q.wait_ge(s.h, v)
            eng.seen[s] = v

    def _commit(self, tok, R, W):
        s, v = tok
        for w in W:
            w.w = tok
            w.r = {}
        for r in R:
            if r.r.get(s, 0) < v:
                r.r[s] = v

    def emit(self, eng, fn, R=(), W=()):
        self._waits(eng, R, W)
        ins = fn()
        eng.sem.cnt += 1
        ins.then_inc(eng.sem.h, 1)
        tok = (eng.sem, eng.sem.cnt)
        self._commit(tok, R, W)
        return tok

    def dma(self, eng, fn, sem, R=(), W=()):
        self._waits(eng, R, W)
        ins = fn()
        sem.cnt += 16
        ins.then_inc(sem.h, 16)
        tok = (sem, sem.cnt)
        self._commit(tok, R, W)
        return tok

    def barrier(self, engs=None):
        engs = engs or [self.PE, self.ACT, self.DVE]
        for e in engs:
            for o in engs:
                if o is e or o.sem.cnt == 0:
                    continue
                if e.seen.get(o.sem, 0) < o.sem.cnt:
                    e.q.wait_ge(o.sem.h, o.sem.cnt)
                    e.seen[o.sem] = o.sem.cnt


def build_nc():
    nc = bass.Bass("TRN2", target_bir_lowering=False)

    def din(name, shape):
        return nc.dram_tensor(name, list(shape), F32, kind="ExternalInput").ap()
    xin = din("xin", [2 * HALF, D])
    ln_mix_g = din("ln_mix_g", [D])
    w_in = din("w_in", [D, 9216])
    a_re = din("s5_a_re", [64, 64])
    a_im = din("s5_a_im", [64, 64])
    log_dt = din("s5_log_dt", [64])
    b_re = din("s5_b_re", [64, 64, 16])
    b_im = din("s5_b_im", [64, 64, 16])
    c_re = din("s5_c_re", [64, 16, 64])
    c_im = din("s5_c_im", [64, 16, 64])
    s5_d = din("s5_d", [64, 16])
    w_glu = din("s5_w_glu", [1024, 1024])
    b_glu = din("s5_b_glu", [1024])
    w_ps5 = din("w_proj_s5", [1024, D])
    lb_logits = din("hgrn_lb_logits", [2, 1024])
    norm_g = din("hgrn_norm_g", [1024])
    w_phg = din("w_proj_hgrn", [1024, D])
    w_out = din("w_out", [D, D])
    ln_ffn_g = din("ln_ffn_g", [D])
    w_up = din("w_up", [D, 2 * DFF])
    conv_w = din("conv_w", [3, 2 * DFF])
    conv_b = din("conv_b", [2 * DFF])
    w_down = din("w_down", [DFF, D])
    ln_fin_g = din("ln_final_g", [D])
    c_ident = din("c_ident", [128, 128])
    c_identf = din("c_identf", [128, 128])
    c_maskbd = din("c_maskbd", [128, 128])
    c_rmask = din("c_rmask", [128, TT])
    c_me2 = din("c_me2", [128, 2])
    c_me1 = din("c_me1", [128, 2])
    c_mk = din("c_mk", [128, 4])
    out = nc.dram_tensor("out", [HALF, D], F32, kind="ExternalOutput").ap()
    s5d = nc.dram_tensor("s5d", [8, 128, 6144], BF16, kind="Internal").ap()
    wsc = nc.dram_tensor("wsc", [160, 128, SLOT], BF16, kind="Internal").ap()

    es = contextlib.ExitStack()
    with es:
        k = KB(nc, es)
        PE, ACT, DVE, POOL, SP = k.PE, k.ACT, k.DVE, k.POOL, k.SP
        E = k.emit

        identb = k.sb("identb", [128, 128], BF16)
        identf = k.sb("identf", [128, 128], F32)
        onesb = k.sb("onesb", [128, 128], BF16)
        maskbd = k.sb("maskbd", [128, 128], F32)
        rmask = k.sb("rmask", [128, TT], F32)
        gmixT = k.sb("gmixT", [128, 16], F32)
        gffnT = k.sb("gffnT", [128, 16], F32)
        gfinB = k.sb("gfinB", [128, D], F32)
        bgluT = k.sb("bgluT", [128, 8], F32)
        ngT = k.sb("ngT", [128, 8], F32)
        lbT = k.sb("lbT", [128, 8], F32)
        omlT = k.sb("omlT", [128, 8], F32)
        lgT = k.sb("lgT", [128, 2, 8], F32)
        DqT = k.sb("DqT", [128, 8], F32)
        cw = k.sb("cw", [128, NFC, 2, 3], F32)
        cb = k.sb("cb", [128, NFC, 2], F32)
        M1 = k.sb("M1", [128, 2, 32], F32)
        M2 = k.sb("M2", [128, 2, 32], F32)
        me2 = k.sb("me2", [128, 2], F32)
        me1 = k.sb("me1", [128, 2], F32)
        mk = k.sb("mk", [128, 4], F32)
        cres = Res()
        csem = k.sem("c")
        cres2 = Res()
        csem2 = k.sem("c2")

        ps = [es.enter_context(nc.psum_tensor(f"ps{i}", [128, 512], F32)) for i in range(6)]
        psr = [Res() for _ in range(6)]
        pb = [es.enter_context(nc.psum_tensor(f"pb{i}", [128, 1024], BF16)) for i in range(2)]
        pbr = [Res(), Res()]
        st = {"ps": 0, "pb": 0, "w": 0, "tmp": 0, "s5c": 0, "pb2": 0}

        def nps():
            i = st["ps"]
            st["ps"] = (i + 1) % NRING
            return ps[i], psr[i]

        def npb():
            i = st["pb"]
            st["pb"] = (i + 1) % 2
            return pb[i], pbr[i]

        def ntmp():
            i = st["tmp"]
            st["tmp"] = (i + 1) % NTMP
            return tmps[:, i, :], tmpr[i]

        greg = {}
        cvres = [Res() for _ in range(8)]
        cvsem = [k.sem("cv") for _ in range(8)]

        def gkey(src, K, ncols):
            return (src.tensor.name, int(src.offset), K, ncols)

        def convert(src, K, ncols, R=()):
            key = gkey(src, K, ncols)
            if key in greg:
                return
            gid = len(greg)
            r = Res()
            greg[key] = (gid, r)
            kc = K // 128
            assert kc * ncols <= SLOT
            j = gid % 8
            dst = wsc[gid, :, 0:kc * ncols].rearrange("p (k n) -> p k n", n=ncols)
            srcv = src.rearrange("(k p) n -> p k n", p=128)
            k.dma(POOL, lambda: nc.gpsimd.dma_start(out=dst, in_=srcv), cvsem[j], R=list(R), W=[r, cvres[j]])

        def wload(src, K, ncols):
            gid, gr = greg[gkey(src, K, ncols)]
            i = st["w"]
            st["w"] = (i + 1) % NSLOT
            kc = K // 128
            dst2 = wring[:, i, 0:kc * ncols]
            k.dma(SP, lambda: nc.sync.dma_start(out=dst2, in_=wsc[gid, :, 0:kc * ncols]), wsem[i], R=[gr], W=[wres[i]])
            return dst2.rearrange("p (k n) -> p k n", n=ncols), wres[i]

        def mm(out_, lhsT, rhs, start, stop, R, W):
            E(PE, lambda: nc.tensor.matmul(out_, lhsT=lhsT, rhs=rhs, start=start, stop=stop), R=R, W=W)

        def act(out_, in_, func, R, W, bias=None, scale=None, accum_out=None):
            kw = {}
            if bias is not None:
                kw["bias"] = bias
            if scale is not None:
                kw["scale"] = scale
            if accum_out is not None:
                kw["accum_out"] = accum_out
            E(ACT, lambda: nc.scalar.activation(out=out_, in_=in_, func=func, **kw), R=R, W=W)

        def tt(out_, in0, in1, op, R, W, eng=None):
            e = eng or DVE
            E(e, lambda: e.q.tensor_tensor(out=out_, in0=in0, in1=in1, op=op), R=R, W=W)

        def ts(out_, in0, s1, s2, op0, op1, R, W):
            if op1 is None:
                E(DVE, lambda: nc.vector.tensor_scalar(out=out_, in0=in0, scalar1=s1, scalar2=None, op0=op0), R=R, W=W)
            else:
                E(DVE, lambda: nc.vector.tensor_scalar(out=out_, in0=in0, scalar1=s1, scalar2=s2, op0=op0, op1=op1), R=R, W=W)

        def stt(out_, in0, scalar, in1, op0, op1, R, W):
            E(DVE, lambda: nc.vector.scalar_tensor_tensor(out=out_, in0=in0, scalar=scalar, in1=in1, op0=op0, op1=op1), R=R, W=W)

        def cdma2(eng, out_, in_, slow=True):
            q = eng.q
            k.dma(eng, lambda: q.dma_start(out=out_, in_=in_, allow_slow_non_contiguous=True), csem2, W=[cres2])

        def cdma(eng, out_, in_, slow=True):
            q = eng.q
            k.dma(eng, lambda: q.dma_start(out=out_, in_=in_, allow_slow_non_contiguous=True), csem, W=[cres])

        cdma(POOL, identb[:], c_ident)
        cdma(SP, identf[:], c_identf)
        cdma(SP, maskbd[:], c_maskbd)
        cdma(SP, rmask[:], c_rmask)
        cdma(SP, me2[:], c_me2)
        cdma(SP, me1[:], c_me1)
        cdma(SP, mk[:], c_mk)
        cdma(SP, gmixT[:], ln_mix_g.rearrange("(k p) -> p k", p=128), slow=True)
        cdma(SP, lgT[:], lb_logits.rearrange("l (k p) -> p l k", p=128), slow=True)

        ss = contextlib.ExitStack()
        with ss:
            def sbs(name, shape, dt=F32, stk=None):
                return (stk or ss).enter_context(nc.sbuf_tensor(name, shape, dt))
            s2 = contextlib.ExitStack()
            are1 = sbs("are1", [128, 32]); aim1 = sbs("aim1", [128, 32]); ldt1 = sbs("ldt1", [128, 32])
            B1 = sbs("B1", [128, 2, 32, 16]); C1 = sbs("C1", [128, 2, 32, 16])
            are2 = sbs("are2", [128, 8, 64], stk=s2); aim2 = sbs("aim2", [128, 8, 64], stk=s2); ldt2s = sbs("ldt2s", [128, 8], stk=s2)
            BT2 = sbs("BT2", [128, 2, 8, 64], stk=s2)
            BN = sbs("BN", [64, 2, 8, 8, 16], stk=s2)
            CL3 = sbs("CL3", [64, 2, 8, 2, 64], stk=s2)
            cdma(SP, are1[:], a_re.rearrange("(pr e) p -> (e p) pr", e=2), slow=True)
            cdma(SP, aim1[:], a_im.rearrange("(pr e) p -> (e p) pr", e=2), slow=True)
            for e in range(2):
                cdma(SP, ldt1[64 * e:64 * e + 64, :], log_dt.rearrange("(pr e) -> e pr", e=2)[e].partition_broadcast(64))
            for ri, (bsrc, csrc) in enumerate(((b_re, c_re), (b_im, c_im))):
                cdma(SP, B1[:, ri, :, :], bsrc.rearrange("(pr e) p c -> (e p) pr c", e=2))
                for r_ in range(4):
                    for e in range(2):
                        cdma(SP, CL3[16 * r_:16 * r_ + 16, ri, :, e, :], csrc.rearrange("(b r e) c p -> r e c b p", r=4, e=2)[r_, e])
                for q in range(8):
                    cdma(SP, BN[:, ri, :, q, :], bsrc.rearrange("(b q) p c -> q p b c", q=8)[q])
            for q in range(8):
                cdma(SP, are2[16 * q:16 * q + 16, :, :], a_re.rearrange("(b q) p -> q b p", q=8)[q].partition_broadcast(16))
                cdma(SP, aim2[16 * q:16 * q + 16, :, :], a_im.rearrange("(b q) p -> q b p", q=8)[q].partition_broadcast(16))
                cdma(SP, ldt2s[16 * q:16 * q + 16, :], log_dt.rearrange("(b q) -> q b", q=8)[q].partition_broadcast(16), slow=True)
            cres.w = (csem, csem.cnt)
            cdma2(SP, gffnT[:], ln_ffn_g.rearrange("(k p) -> p k", p=128), slow=True)
            cdma2(SP, gfinB[:], ln_fin_g.partition_broadcast(128))
            cdma2(SP, bgluT[:], b_glu.rearrange("(k p) -> p k", p=128), slow=True)
            cdma2(SP, ngT[:], norm_g.rearrange("(k p) -> p k", p=128), slow=True)
            cdma2(SP, DqT[:], s5_d.rearrange("(b q) c -> (q c) b", q=8), slow=True)
            for v_ in range(2):
                for j_ in range(3):
                    cdma2(SP, cw[:, :, v_, j_], conv_w[j_, v_ * DFF:(v_ + 1) * DFF].rearrange("(c p) -> p c", p=128), slow=True)
                cdma2(SP, cb[:, :, v_], conv_b[v_ * DFF:(v_ + 1) * DFF].rearrange("(c p) -> p c", p=128), slow=True)
            cres2.w = (csem2, csem2.cnt)

            C = [cres]
            for g4 in range(4):
                convert(w_in[:, g4 * 256:(g4 + 1) * 256], D, 256)
            for hd in range(8):
                convert(w_in[:, 2048 + hd * 128:2048 + (hd + 1) * 128], D, 128)
                convert(w_in[:, 3072 + hd * 128:3072 + (hd + 1) * 128], D, 128)
            for ri in range(2):
                pk, pkr = nps()
                for b in range(8):
                    E(PE, lambda: nc.tensor.transpose(out=pk[:, b * 64:(b + 1) * 64], in_=CL3[:, ri, b].rearrange("p e x -> p (e x)"),
                                                      identity=identf[0:64, 0:64]), R=C, W=[pkr])
                act(C1[:, ri].rearrange("p r c -> p (r c)"), pk[:, 0:512], AF.Copy, R=[pkr], W=C)
                pk, pkr = nps()
                for b in range(8):
                    E(PE, lambda: nc.tensor.transpose(out=pk[:, b * 64:(b + 1) * 64], in_=BN[:, ri, b].rearrange("p q c -> p (q c)"),
                                                      identity=identf[0:64, 0:64]), R=C, W=[pkr])
                act(BT2[:, ri].rearrange("p b x -> p (b x)"), pk[:, 0:512], AF.Copy, R=[pkr], W=C)
            E(DVE, lambda: nc.vector.memset(onesb[:], 1.0), W=C)
            tt(lbT[:], lgT[:, 1, :], lgT[:, 0, :], ALU.subtract, R=C, W=C)
            act(lbT[:], lbT[:], AF.Exp, R=C, W=C)
            ts(lbT[:], lbT[:], 1.0, None, ALU.add, None, R=C, W=C)
            E(DVE, lambda: nc.vector.reciprocal(out=lbT[:], in_=lbT[:]), R=C, W=C)
            ts(omlT[:], lbT[:], -1.0, 1.0, ALU.mult, ALU.add, R=C, W=C)

            def s5_scalars(tag, are, aim, ldt, shp, stk, npow):
                fs = [128] + shp
                nm = [0]
                tstk = contextlib.ExitStack()

                def P(dt=F32):
                    nm[0] += 1
                    return sbs(f"{tag}{nm[0]}", fs, dt, stk=stk)
                one = P(); zero = P(); cre = P(); cim = P()
                pws = [(P(), P()) for _ in range(npow)]

                def T(dt=F32):
                    nm[0] += 1
                    return sbs(f"{tag}{nm[0]}", fs, dt, stk=tstk)
                lre = T(); dt_ = T(); xr_ = T(); an = T()
                ts(lre[:], are, -1e-4, None, ALU.min, None, R=C, W=C)
                act(dt_[:], ldt, AF.Exp, R=C, W=C)
                tt(xr_[:], lre[:], dt_[:], ALU.mult, R=C, W=C)
                tt(an[:], aim, dt_[:], ALU.mult, R=C, W=C)
                pw = []
                E(DVE, lambda: nc.vector.memset(one[:], 1.0), W=C)
                E(DVE, lambda: nc.vector.memset(zero[:], 0.0), W=C)
                pw.append((one, zero))
                t_ = T(); ti = T(I32); tf = T(); r_ = T(); s1 = T(); s2 = T(); c2 = T(); mag = T()
                for j in range(1, npow + 1):
                    pre, pim = pws[j - 1]
                    act(mag[:], xr_[:], AF.Exp, R=C, W=C, scale=float(j))
                    ts(t_[:], an[:], float(j) / (2 * math.pi), None, ALU.mult, None, R=C, W=C)
                    E(DVE, lambda: nc.vector.tensor_copy(out=ti[:], in_=t_[:]), R=C, W=C)
                    E(DVE, lambda: nc.vector.tensor_copy(out=tf[:], in_=ti[:]), R=C, W=C)
                    tt(r_[:], t_[:], tf[:], ALU.subtract, R=C, W=C)
                    act(s2[:], r_[:], AF.Sin, R=C, W=C, scale=math.pi)
                    act(s1[:], r_[:], AF.Sin, R=C, W=C, scale=math.pi / 2)
                    tt(c2[:], s1[:], s1[:], ALU.mult, R=C, W=C)
                    ts(c2[:], c2[:], -2.0, 1.0, ALU.mult, ALU.add, R=C, W=C)
                    tt(pim[:], s2[:], c2[:], ALU.mult, R=C, W=C)
                    ts(pim[:], pim[:], 2.0, None, ALU.mult, None, R=C, W=C)
                    tt(pre[:], s2[:], s2[:], ALU.mult, R=C, W=C)
                    ts(pre[:], pre[:], -2.0, 1.0, ALU.mult, ALU.add, R=C, W=C)
                    tt(pre[:], pre[:], mag[:], ALU.mult, R=C, W=C)
                    tt(pim[:], pim[:], mag[:], ALU.mult, R=C, W=C)
                    pw.append((pre, pim))
                den = T(); nr = T(); t2 = T()
                tt(den[:], lre[:], lre[:], ALU.mult, R=C, W=C)
                tt(t2[:], aim, aim, ALU.mult, R=C, W=C)
                tt(den[:], den[:], t2[:], ALU.add, R=C, W=C)
                E(DVE, lambda: nc.vector.reciprocal(out=den[:], in_=den[:]), R=C, W=C)
                ts(nr[:], pw[1][0][:], -1.0, None, ALU.add, None, R=C, W=C)
                ni = pw[1][1]
                tt(cre[:], nr[:], lre[:], ALU.mult, R=C, W=C)
                tt(t2[:], ni[:], aim, ALU.mult, R=C, W=C)
                tt(cre[:], cre[:], t2[:], ALU.add, R=C, W=C)
                tt(cre[:], cre[:], den[:], ALU.mult, R=C, W=C)
                tt(cim[:], ni[:], lre[:], ALU.mult, R=C, W=C)
                tt(t2[:], nr[:], aim, ALU.mult, R=C, W=C)
                tt(cim[:], cim[:], t2[:], ALU.subtract, R=C, W=C)
                tt(cim[:], cim[:], den[:], ALU.mult, R=C, W=C)
                tstk.close()
                return pw, cre, cim

            pw2, cre2, cim2 = s5_scalars("b", are2[:], aim2[:], ldt2s[:].rearrange("p (b o) -> p b o", o=1).broadcast_to([128, 8, 64]), [8, 64], s2, 7)

            WA = sbs("WA", [128, 8, 8, 2, 2, 64], BF16, stk=s2)
            Bc2r = sbs("Bc2r", [128, 8, 64], stk=s2); Bc2i = sbs("Bc2i", [128, 8, 64], stk=s2); u1 = sbs("u1", [128, 8, 64], stk=s2); u2 = sbs("u2", [128, 8, 64], stk=s2)
            tt(Bc2r[:], cre2[:], BT2[:, 0], ALU.mult, R=C, W=C)
            tt(u1[:], cim2[:], BT2[:, 1], ALU.mult, R=C, W=C)
            tt(Bc2r[:], Bc2r[:], u1[:], ALU.subtract, R=C, W=C)
            tt(Bc2i[:], cre2[:], BT2[:, 1], ALU.mult, R=C, W=C)
            tt(u1[:], cim2[:], BT2[:, 0], ALU.mult, R=C, W=C)
            tt(Bc2i[:], Bc2i[:], u1[:], ALU.add, R=C, W=C)
            for i in range(8):
                Ar, Ai = pw2[7 - i]
                tt(u1[:], Ar[:], Bc2r[:], ALU.mult, R=C, W=C)
                tt(u2[:], Ai[:], Bc2i[:], ALU.mult, R=C, W=C)
                tt(u1[:], u1[:], u2[:], ALU.subtract, R=C, W=C)
                for e in range(2):
                    ts(WA[:, :, i, 0, e, :], u1[:], me2[:, e:e + 1], None, ALU.mult, None, R=C, W=C)
                tt(u1[:], Ar[:], Bc2i[:], ALU.mult, R=C, W=C)
                tt(u2[:], Ai[:], Bc2r[:], ALU.mult, R=C, W=C)
                tt(u1[:], u1[:], u2[:], ALU.add, R=C, W=C)
                for e in range(2):
                    ts(WA[:, :, i, 1, e, :], u1[:], me2[:, e:e + 1], None, ALU.mult, None, R=C, W=C)

            for b in range(8):
                k.dma(SP, lambda: nc.sync.dma_start(out=s5d[b, :, 0:2048], in_=WA[:, b].rearrange("p i r e q -> p (i r e q)")), csem, R=C, W=C)
            for e_ in (PE, ACT, DVE, SP, POOL):
                k._waits(e_, [cres], [cres])
            k.barrier([PE, ACT, DVE, SP, POOL])
            s2.close()
            pw1, cre1, cim1 = s5_scalars("a", are1[:], aim1[:], ldt1[:], [32], ss, 8)
            E(DVE, lambda: nc.vector.tensor_copy(out=M1[:, 0, :], in_=pw1[8][0][:]), R=C, W=C)
            E(DVE, lambda: nc.vector.tensor_copy(out=M1[:, 1, :], in_=pw1[8][0][:]), R=C, W=C)
            ts(M2[:, 0, :], pw1[8][1][:], -1.0, None, ALU.mult, None, R=C, W=C)
            E(DVE, lambda: nc.vector.tensor_copy(out=M2[:, 1, :], in_=pw1[8][1][:]), R=C, W=C)

            WC = sbs("WC", [128, 32, 9, 2, 2, 16], BF16)
            BP = sbs("BP", [128, 32, 2, 2, 16], BF16)
            v1 = sbs("v1", [128, 32, 16]); v2 = sbs("v2", [128, 32, 16]); Bc1r = sbs("Bc1r", [128, 32, 16]); Bc1i = sbs("Bc1i", [128, 32, 16])

            def bc16(t):
                return t[:].rearrange("p (f o) -> p f o", o=1).broadcast_to([128, 32, 16])
            for m in range(9):
                Ar, Ai = pw1[m]
                tt(v1[:], C1[:, 0], bc16(Ar), ALU.mult, R=C, W=C)
                tt(v2[:], C1[:, 1], bc16(Ai), ALU.mult, R=C, W=C)
                tt(v1[:], v1[:], v2[:], ALU.subtract, R=C, W=C)
                for e in range(2):
                    ts(WC[:, :, m, 0, e, :], v1[:], me1[:, e:e + 1], None, ALU.mult, None, R=C, W=C)
                tt(v1[:], C1[:, 0], bc16(Ai), ALU.mult, R=C, W=C)
                tt(v2[:], C1[:, 1], bc16(Ar), ALU.mult, R=C, W=C)
                tt(v1[:], v1[:], v2[:], ALU.add, R=C, W=C)
                ts(v1[:], v1[:], -1.0, None, ALU.mult, None, R=C, W=C)
                for e in range(2):
                    ts(WC[:, :, m, 1, e, :], v1[:], me1[:, e:e + 1], None, ALU.mult, None, R=C, W=C)
            tt(Bc1r[:], bc16(cre1), B1[:, 0], ALU.mult, R=C, W=C)
            tt(v1[:], bc16(cim1), B1[:, 1], ALU.mult, R=C, W=C)
            tt(Bc1r[:], Bc1r[:], v1[:], ALU.subtract, R=C, W=C)
            tt(Bc1i[:], bc16(cre1), B1[:, 1], ALU.mult, R=C, W=C)
            tt(v1[:], bc16(cim1), B1[:, 0], ALU.mult, R=C, W=C)
            tt(Bc1i[:], Bc1i[:], v1[:], ALU.add, R=C, W=C)
            for e in range(2):
                ts(BP[:, :, 0, e, :], Bc1r[:], me1[:, e:e + 1], None, ALU.mult, None, R=C, W=C)
                ts(BP[:, :, 1, e, :], Bc1i[:], me1[:, e:e + 1], None, ALU.mult, None, R=C, W=C)

            KBD = sbs("KBD", [128, 8, 8, 128], BF16)
            E(DVE, lambda: nc.vector.memset(KBD[:], 0.0), W=C)
            BP64 = sbs("BP64", [128, 32, 2, 2, 32], BF16)
            E(DVE, lambda: nc.vector.memset(BP64[:], 0.0), W=C)
            BPv = BP[:].rearrange("p (b r) i e c -> p b r i (e c)", r=4)
            BP64v = BP64[:].rearrange("p (b r) i s c -> p b r i s c", r=4)
            for r in (2, 3):
                E(DVE, lambda: nc.vector.tensor_copy(out=BP64v[:, :, r, :, r - 2, :], in_=BPv[:, :, r, :, :]), R=C, W=C)
            for b in range(8):
                pk, pkr = nps()
                for r in range(4):
                    pr = 4 * b + r
                    if r < 2:
                        o_ = pk[32 * r:32 * r + 32, 0:256].rearrange("p (t c) -> p t c", c=32)
                    else:
                        o_ = pk[64:128, (r - 2) * 256:(r - 1) * 256].rearrange("p (t c) -> p t c", c=32)
                    for ri in range(2):
                        if r < 2:
                            lhs = BP[:, pr, ri, :, :].rearrange("p e c -> p (e c)")
                        else:
                            lhs = BP64[:, pr, ri, :, :].rearrange("p s c -> p (s c)")
                        rhs = WC[:, pr, 0:8, ri, :, :].rearrange("p t e c -> p t (e c)")
                        mm(o_, lhs, rhs, ri == 0, ri == 1, R=C, W=[pkr])
                for r in range(4):
                    if r < 2:
                        o_ = pk[32 * r:32 * r + 32, 0:256].rearrange("p (t c) -> p t c", c=32)
                        act(KBD[32 * r:32 * r + 32, b, :, 32 * r:32 * r + 32], o_, AF.Copy, R=[pkr], W=C)
                    else:
                        o_ = pk[64:128, (r - 2) * 256:(r - 1) * 256].rearrange("p (t c) -> p t c", c=32)
                        act(KBD[64:128, b, :, 32 * r:32 * r + 32], o_, AF.Copy, R=[pkr], W=C)
            WC64 = sbs("WC64", [128, 8, 4, 8, 2, 64], BF16)
            scr = Res()
            E(DVE, lambda: nc.vector.memset(WC64[:, :, 2, :, :, 32:64], 0.0), R=C, W=C)
            E(DVE, lambda: nc.vector.memset(WC64[:, :, 3, :, :, 0:32], 0.0), R=C, W=C)
            WCv = WC[:].rearrange("p (b r) m i e c -> p b r m i (e c)", r=4)
            for r in range(4):
                c0 = 32 if r == 3 else 0
                for ri in range(2):
                    E(DVE, lambda: nc.vector.tensor_copy(out=WC64[:, :, r, :, ri, c0:c0 + 32], in_=WCv[:, :, r, 1:9, ri, :]), R=C, W=C)
            for b in range(8):
                k.dma(SP, lambda: nc.sync.dma_start(out=s5d[b, :, 2048:3072].rearrange("p (x c) -> p x c", c=32),
                                                    in_=WC64[:, b, 0:2, :, :, 0:32].rearrange("p r j i c -> p (r j i) c")), csem, R=C, W=[scr])
                k.dma(SP, lambda: nc.sync.dma_start(out=s5d[b, :, 3072:5120], in_=WC64[:, b, 2:4].rearrange("p r j i c -> p (r j i c)")), csem, R=C, W=[scr])
                k.dma(SP, lambda: nc.sync.dma_start(out=s5d[b, :, 5120:6144], in_=KBD[:, b].rearrange("p t c -> p (t c)")), csem, R=C, W=[scr])
            cres.w = (csem, csem.cnt)
            for e_ in (PE, ACT, DVE, SP, POOL):
                k._waits(e_, [cres], [cres])
            k.barrier([PE, ACT, DVE, SP, POOL])
        C = [cres, cres2]
        xres = k.sb("xres", [128, 2, NSUB, D], F32)
        xr = [[Res() for _ in range(NSUB)] for _ in range(2)]
        xsem = [[k.sem("x") for _ in range(NSUB)] for _ in range(2)]
        hT = k.sb("hT", [128, 16, TT], BF16)
        hTr = [Res() for _ in range(NSUB)]
        wring = k.sb("wring", [128, NSLOT, SLOT], BF16)
        wres = [Res() for _ in range(NSLOT)]
        wsem = [k.sem("w") for _ in range(NSLOT)]
        s5a = k.sb("s5a", [128, 2, 2048], BF16)
        s5ar = [Res(), Res()]
        s5asem = [k.sem("s5a"), k.sem("s5a")]
        s5cc = k.sb("s5cc", [128, 2, 4096], BF16)
        s5ccr = [Res(), Res()]
        s5ccsem = [k.sem("s5cc"), k.sem("s5cc")]
        uTm = k.sb("uTm", [128, 4, 8, TT], BF16)
        uTr = [Res() for _ in range(8)]
        mT = uTm[:, 0:2].rearrange("p a b t -> p (a b) t")
        mTr = uTr
        hist = k.sb("hist", [128, 2, 32, NK8 + 1], F32)
        histr = Res()
        arena = k.sb("arena", [128, NFC * TT], BF16)
        actT = arena[:, :].rearrange("p (c t) -> p c t", t=TT)
        uT = arena[:, 0:8 * TT].rearrange("p (b t) -> p b t", t=TT)
        histb = arena[:, 8 * TT:8 * TT + 64 * NK8].rearrange("p (i r k) -> p i r k", i=2, r=32)
        o1 = 8 * TT + 64 * NK8
        zT = arena[:, o1:o1 + 8 * TT].rearrange("p (b t) -> p b t", t=TT)
        s5o = arena[:, o1 + 8 * TT:o1 + 16 * TT].rearrange("p (b t) -> p b t", t=TT)
        ohT = arena[:, o1 + 16 * TT:o1 + 24 * TT].rearrange("p (b t) -> p b t", t=TT)
        assert o1 + 24 * TT <= NFC * TT
        histbr = Res(); zTr = Res(); s5or = Res(); ohTr = Res(); actTr = Res()
        actTR = [actTr] + uTr + [histbr, zTr, s5or, ohTr]
        tmps = k.sb("tmps", [128, NTMP, TT + 2], F32)
        tmpr = [Res() for _ in range(NTMP)]
        xn = k.sb("xn", [128, NSUB, D], BF16)
        xnr = [Res() for _ in range(NSUB)]
        ot = k.sb("ot", [128, D // 2], F32)
        otr = Res()
        osem = k.sem("o")
        S32 = k.sb("S32", [128, 8, 128], F32)
        Sb = k.sb("Sb", [128, 8, 128], BF16)
        Sr = [Res() for _ in range(8)]
        Sbr = [Res() for _ in range(8)]
        qdT = k.sb("qdT", [128, 2, TT], BF16)
        qdTr = [Res(), Res()]
        kdT = k.sb("kdT", [128, 2, TT], BF16)
        kdTr = [Res(), Res()]
        kdtok = k.sb("kdtok", [128, 2, 2, NSUB, 128], BF16)
        kdtokr = [Res(), Res()]
        vtok = k.sb("vtok", [128, 2, NSUB, 128], BF16)
        vtokr = [Res(), Res()]
        Efb = k.sb("Efb", [128, 2, TT], F32)
        Efr = [Res(), Res()]
        sggb = k.sb("sggb", [128, 2, TT], BF16)
        sggr = [Res(), Res()]
        scT = k.sb("scT", [128, 128], BF16)
        scTr = Res()
        sqo = k.sb("sqo", [128, 128], BF16)
        sqor = Res()
        chist = k.sb("chist", [128, NFC, 2, 2], F32)
        chr_ = [Res() for _ in range(NFC)]
        stat = k.sb("stat", [128, 8], F32)
        sbt = k.sb("sbt", [128, 2, 2, 2, 32], F32)
        sbtr = [[Res(), Res()], [Res(), Res()]]
        statr = Res()
        E(DVE, lambda: nc.vector.memset(S32[:], 0.0), W=Sr)
        E(DVE, lambda: nc.vector.memset(Sb[:], 0.0), W=Sbr)
        E(DVE, lambda: nc.vector.memset(hist[:], 0.0), W=[histr])
        E(DVE, lambda: nc.vector.memset(chist[:], 0.0), W=chr_)

        def load_s5a(b):
            i = st["s5c"]
            st["s5c"] = 1 - i
            k.dma(SP, lambda: nc.sync.dma_start(out=s5a[:, i, :], in_=s5d[b, :, 0:2048]), s5asem[i], R=C, W=[s5ar[i]])
            wa = s5a[:, i, :].rearrange("p (i r c) -> p i r c", i=8, r=2)
            return wa, s5ar[i]

        def load_s5cc(b):
            i = st["pb2"]
            st["pb2"] = 1 - i
            k.dma(SP, lambda: nc.sync.dma_start(out=s5cc[:, i, :], in_=s5d[b, :, 2048:6144]), s5ccsem[i], R=C, W=[s5ccr[i]])
            v = s5cc[:, i, :]
            wc32 = v[:, 0:1024].rearrange("p (r j i c) -> p r j i c", r=2, j=8, i=2)
            wc64 = v[:, 1024:3072].rearrange("p (r j i c) -> p r j i c", r=2, j=8, i=2)
            kb = v[:, 3072:4096].rearrange("p (t c) -> p t c", t=8)
            return wc32, wc64, kb, s5ccr[i]

        def norm_a(xb):
            for sub in range(NSUB):
                act(xn[:, sub, :], xres[:, xb, sub, :], AF.Square, R=[xr[xb][sub]], W=[xnr[sub], statr], accum_out=stat[:, 0:1])
                act(stat[:, 1:2], stat[:, 0:1], AF.Ln, R=[statr], W=[statr], scale=1.0 / D, bias=EPS)
                act(stat[:, 2:3], stat[:, 1:2], AF.Exp, R=[statr], W=[statr], scale=-0.5)
                ts(xn[:, sub, :], xres[:, xb, sub, :], stat[:, 2:3], None, ALU.mult, None, R=[xr[xb][sub], statr], W=[xnr[sub]])

        def norm_b(gT):
            for sub in range(NSUB):
                for h8 in range(2):
                    p_, p_r = npb()
                    for kk in range(8):
                        kc = h8 * 8 + kk
                        E(PE, lambda: nc.tensor.transpose(out=p_[:, kk * 128:(kk + 1) * 128], in_=xn[:, sub, kc * 128:(kc + 1) * 128], identity=identb[:]),
                          R=[xnr[sub]] + C, W=[p_r])
                    tt(hT[:, h8 * 8:h8 * 8 + 8, sub * 128:(sub + 1) * 128], p_[:].rearrange("p (k t) -> p k t", t=128),
                       gT[:, h8 * 8:h8 * 8 + 8].rearrange("p (k o) -> p k o", o=1).broadcast_to([128, 8, 128]), ALU.mult,
                       R=[p_r] + C, W=[hTr[sub]])

        def rmsnorm_to_hT(gT, xb):
            norm_a(xb)
            norm_b(gT)

        def fm_group(wsb, wr, kc_n, col0, actT_, actR):
            p_, p_r = nps()
            for kk in range(kc_n):
                mm(p_[:, 0:TT], wsb[:, kk, col0:col0 + 128], actT_[:, kk, :], kk == 0, kk == kc_n - 1, R=[wr] + actR, W=[p_r])
            return p_, p_r

        late_convs = []
        for g4 in range(4):
            late_convs.append((w_glu[:, g4 * 256:(g4 + 1) * 256], 1024, 256))
        for hd in range(8):
            late_convs.append((w_in[:, 1024 + hd * 128:1024 + (hd + 1) * 128], D, 128))
            late_convs.append((w_in[:, 4096 + hd * 128:4096 + (hd + 1) * 128], D, 128))
        for m2 in range(8):
            late_convs.append((w_ps5[:, m2 * 256:(m2 + 1) * 256], 1024, 256))
            late_convs.append((w_in[:, 5120 + m2 * 256:5120 + (m2 + 1) * 256], D, 256))
            late_convs.append((w_phg[:, m2 * 256:(m2 + 1) * 256], 1024, 256))
            late_convs.append((w_in[:, 7168 + m2 * 256:7168 + (m2 + 1) * 256], D, 256))
        for n in range(8):
            late_convs.append((w_out[:, n * 256:(n + 1) * 256], D, 256))
        for c2 in range(NFC // 2):
            late_convs.append((w_up[:, c2 * 256:(c2 + 1) * 256], D, 256))
            late_convs.append((w_up[:, DFF + c2 * 256:DFF + (c2 + 1) * 256], D, 256))
        for n in range(8):
            kg = 0
            for grp in (16, 16, 12):
                late_convs.append((w_down[kg * 128:(kg + grp) * 128, n * 256:(n + 1) * 256], grp * 128, 256))
                kg += grp


        def xload(tn):
            if tn >= NTILE:
                return
            for sub in range(NSUB):
                r0 = tn * TT + sub * 128
                k.dma(ACT, lambda: nc.scalar.dma_start(out=xres[:, tn % 2, sub, :], in_=xin[r0:r0 + 128, :]), xsem[tn % 2][sub], W=[xr[tn % 2][sub]])

        def epilogue(te, xe):
            for sub in range(NSUB):
                act(xn[:, sub, :], xres[:, xe, sub, :], AF.Square, R=[xr[xe][sub]], W=[xnr[sub], statr], accum_out=stat[:, 3:4])
                act(stat[:, 4:5], stat[:, 3:4], AF.Ln, R=[statr], W=[statr], scale=1.0 / D, bias=EPS)
                act(stat[:, 5:6], stat[:, 4:5], AF.Exp, R=[statr], W=[statr], scale=-0.5)
                r0 = (te - NWARM) * TT + sub * 128
                for hh in range(2):
                    hs = slice(hh * (D // 2), (hh + 1) * (D // 2))
                    ts(ot[:], xres[:, xe, sub, hs], stat[:, 5:6], None, ALU.mult, None, R=[xr[xe][sub], statr], W=[otr])
                    tt(ot[:], ot[:], gfinB[:, hs], ALU.mult, R=[otr] + C, W=[otr])
                    k.dma(ACT, lambda: nc.scalar.dma_start(out=out[r0:r0 + 128, hs], in_=ot[:]), osem, R=[otr], W=[otr])

        xload(0)
        for ci, cv in enumerate(late_convs):
            convert(*cv, R=(xr[0] if ci == 0 else ()))
        rmsnorm_to_hT(gmixT, 0)
        pend_epi = None
        for t in range(NTILE):
            full = t >= NWARM - 1
            main = t >= NWARM
            xb = t % 2
            has_next = t + 1 < NTILE
            if not full:
                xload(t + 1)

            for g4 in range(4):
                wsb, wr = wload(w_in[:, g4 * 256:(g4 + 1) * 256], D, 256)
                for j in range(2):
                    p_, p_r = fm_group(wsb, wr, 16, j * 128, hT, hTr)
                    act(uT[:, g4 * 2 + j, :], p_[:, 0:TT], AF.Copy, R=[p_r], W=[uTr[g4 * 2 + j]])
                    for v_ in range(4):
                        if v_ % 2 == 0:
                            ts(uTm[:, v_, g4 * 2 + j, :], p_[:, 0:TT], mk[:, v_:v_ + 1], None, ALU.mult, None, R=[p_r] + C, W=[uTr[g4 * 2 + j]])
                        else:
                            act(uTm[:, v_, g4 * 2 + j, :], p_[:, 0:TT], AF.Identity, R=[p_r] + C, W=[uTr[g4 * 2 + j]], scale=mk[:, v_:v_ + 1])
            for b in range(8):
                wa, cr = load_s5a(b)
                p_, p_r = nps()
                for r in range(4):
                    for ri in range(2):
                        o_ = p_[:, (r * 2 + ri) * NK8:(r * 2 + ri + 1) * NK8]
                        for i in range(8):
                            rhs = uTm[:, r, b, :].rearrange("p (k j) -> p k j", j=8)[:, :, i]
                            mm(o_, wa[:, i, ri, :], rhs, i == 0, i == 7, R=[cr, uTr[b]], W=[p_r])
                for ri in range(2):
                    act(hist[:, ri, 4 * b:4 * b + 4, 1:NK8 + 1],
                        p_[:, 0:8 * NK8].rearrange("p (r i k) -> p i r k", r=4, i=2)[:, ri], AF.Copy, R=[p_r], W=[histr])
            def stageB(k0, k1):
                sbe = POOL if full else DVE
                for kk in range(k0, k1):
                    t1v, t1r = sbt[:, kk % 2, 0], sbtr[kk % 2][0]
                    t2v, t2r = sbt[:, kk % 2, 1], sbtr[kk % 2][1]
                    tt(t1v, M1[:], hist[:, :, :, kk], ALU.mult, R=[histr] + C, W=[t1r], eng=sbe)
                    tt(t2v[:, 0, :], M2[:, 0, :], hist[:, 1, :, kk], ALU.mult, R=[histr] + C, W=[t2r], eng=sbe)
                    tt(t2v[:, 1, :], M2[:, 1, :], hist[:, 0, :, kk], ALU.mult, R=[histr] + C, W=[t2r], eng=sbe)
                    tt(t1v, t1v, t2v, ALU.add, R=[t1r, t2r], W=[t1r], eng=sbe)
                    tt(hist[:, :, :, kk + 1], hist[:, :, :, kk + 1], t1v, ALU.add, R=[t1r, histr], W=[histr], eng=sbe)

            def hgrn_s1_gen(hd):
                par = hd % 2
                wsb, wr = wload(w_in[:, 2048 + hd * 128:2048 + (hd + 1) * 128], D, 128)
                p_, p_r = fm_group(wsb, wr, 16, 0, hT, hTr)
                f_, f_r = ntmp()
                act(f_[:, 0:TT], p_[:, 0:TT], AF.Sigmoid, R=[p_r], W=[f_r])
                ts(f_[:, 0:TT], f_[:, 0:TT], omlT[:, hd:hd + 1], lbT[:, hd:hd + 1], ALU.mult, ALU.add, R=[f_r] + C, W=[f_r])
                lf, lfr = ntmp()
                act(lf[:, 0:TT], f_[:, 0:TT], AF.Ln, R=[f_r], W=[lfr])
                bc, bcr = ntmp()
                E(DVE, lambda: nc.vector.tensor_tensor_scan(out=bc[:, 0:TT], data0=rmask[:], data1=lf[:, 0:TT], initial=0.0,
                                                            op0=ALU.mult, op1=ALU.add), R=[lfr] + C, W=[bcr])
                act(Efb[:, par, :], bc[:, 0:TT], AF.Exp, R=[bcr], W=[Efr[par]])
                Ei, Eir = ntmp()
                act(Ei[:, 0:TT], bc[:, 0:TT], AF.Exp, R=[bcr], W=[Eir], scale=-1.0)
                ts(f_[:, 0:TT], f_[:, 0:TT], -1.0, 1.0, ALU.mult, ALU.add, R=[f_r], W=[f_r])
                tt(kdT[:, par, :], f_[:, 0:TT], Ei[:, 0:TT], ALU.mult, R=[f_r, Eir], W=[kdTr[par]])
                yield
                wsb, wr = wload(w_in[:, 3072 + hd * 128:3072 + (hd + 1) * 128], D, 128)
                p_, p_r = nps()
                for sub in range(NSUB):
                    for kk in range(16):
                        mm(p_[:, sub * 128:(sub + 1) * 128], hT[:, kk, sub * 128:(sub + 1) * 128], wsb[:, kk, :], kk == 0, kk == 15,
                           R=[wr] + hTr, W=[p_r])
                    if sub == NSUB - 1:
                        act(vtok[:, par], p_[:, 0:NSUB * 128].rearrange("p (s c) -> p s c", c=128), AF.Copy, R=[p_r], W=[vtokr[par]])
                    yield
                if full:
                    wsb, wr = wload(w_in[:, 1024 + hd * 128:1024 + (hd + 1) * 128], D, 128)
                    p_, p_r = fm_group(wsb, wr, 16, 0, hT, hTr)
                    sq, sqr = ntmp()
                    act(sq[:, 0:TT], p_[:, 0:TT], AF.Silu, R=[p_r], W=[sqr])
                    tt(qdT[:, par, :], sq[:, 0:TT], Efb[:, par, :], ALU.mult, R=[sqr, Efr[par]], W=[qdTr[par]])
                    yield
                    wsb, wr = wload(w_in[:, 4096 + hd * 128:4096 + (hd + 1) * 128], D, 128)
                    p_, p_r = fm_group(wsb, wr, 16, 0, hT, hTr)
                    act(sggb[:, par, :], p_[:, 0:TT], AF.Silu, R=[p_r], W=[sggr[par]])
                    yield

            def hgrn_s2a(hd):
                par = hd % 2
                pb_, pb_r = npb()
                for sub in range(NSUB):
                    E(PE, lambda: nc.tensor.transpose(out=pb_[:, sub * 128:(sub + 1) * 128], in_=kdT[:, par, sub * 128:(sub + 1) * 128], identity=identb[:]),
                      R=[kdTr[par]] + C, W=[pb_r])
                for hf_ in range(2):
                    act(kdtok[:, par, hf_], pb_[:, 0:NSUB * 128].rearrange("p (s c) -> p s c", c=128), AF.Identity,
                        R=[pb_r] + C, W=[kdtokr[par]], scale=me1[:, hf_:hf_ + 1])

            def hgrn_s2b_gen(hd):
                par = hd % 2
                po, por = ps[5], psr[5]
                for sub in range(NSUB):
                    tok = slice(sub * 128, (sub + 1) * 128)
                    if full:
                        psc, pscr = nps()
                        mm(psc[:, 0:128], kdT[:, par, tok], qdT[:, par, tok], True, True, R=[kdTr[par], qdTr[par]], W=[pscr])
                        tt(scT[:], psc[:, 0:128], maskbd[:], ALU.mult, R=[pscr] + C, W=[scTr])
                        yield
                        mm(po[:, 0:128], vtok[:, par, sub, :], scT[:], True, False, R=[vtokr[par], scTr], W=[por])
                    for hf in range(2):
                        c0 = sub * 128 + hf * 64
                        if full:
                            mm(po[:, hf * 64:hf * 64 + 64], Sb[:, hd, :], qdT[:, par, c0:c0 + 64], False, hf == 1, R=[Sbr[hd], qdTr[par]], W=[por])
                        ebl = Efb[:, par, c0 + 63:c0 + 64]
                        ts(S32[:, hd, :], S32[:, hd, :], ebl, None, ALU.mult, None, R=[Sr[hd], Efr[par]], W=[Sr[hd]])
                        pS, pSr = nps()
                        mm(pS[:, 0:128], kdtok[:, par, hf, sub, :], vtok[:, par, sub, :], True, True, R=[kdtokr[par], vtokr[par]], W=[pSr])
                        if full or t == NWARM - 2:
                            stt(Sb[:, hd, :], pS[:, 0:128], ebl, S32[:, hd, :], ALU.mult, ALU.add, R=[pSr, Sr[hd], Efr[par]], W=[Sbr[hd]])
                        stt(S32[:, hd, :], pS[:, 0:128], ebl, S32[:, hd, :], ALU.mult, ALU.add, R=[pSr, Sr[hd], Efr[par]], W=[Sr[hd]])
                        if full:
                            if hf == 1:
                                act(sqo[:], po[:, 0:128], AF.Square, R=[por], W=[sqor])
                            yield
                    if full:
                        pn, pnr = nps()
                        mm(pn[:, 0:128], onesb[:], sqo[:], True, True, R=[sqor] + C, W=[pnr])
                        rs, rsr = ntmp()
                        act(rs[:, 0:128], pn[:, 0:128], AF.Ln, R=[pnr], W=[rsr], scale=1.0 / 128, bias=EPS)
                        act(rs[:, 0:128], rs[:, 0:128], AF.Exp, R=[rsr], W=[rsr], scale=-0.5)
                        tt(rs[:, 0:128], po[:, 0:128], rs[:, 0:128], ALU.mult, R=[por, rsr], W=[rsr])
                        act(rs[:, 0:128], rs[:, 0:128], AF.Identity, R=[rsr] + C, W=[rsr], scale=ngT[:, hd:hd + 1])
                        tt(ohT[:, hd, tok], rs[:, 0:128], sggb[:, par, tok], ALU.mult, R=[rsr, sggr[par]], W=[ohTr])

            for _ in hgrn_s1_gen(0):
                pass
            for hd in range(8):
                if hd == 4 and not full and has_next:
                    norm_a((t + 1) % 2)
                hgrn_s2a(hd)
                gfill = hgrn_s1_gen(hd + 1) if hd + 1 < 8 else iter(())
                if not full:
                    for _ in gfill:
                        pass
                for _ in hgrn_s2b_gen(hd):
                    next(gfill, None)
                for _ in gfill:
                    pass
                stageB(hd * NK8 // 8, (hd + 1) * NK8 // 8)
            if full:
                E(POOL, lambda: nc.gpsimd.tensor_copy(out=histb, in_=hist[:, :, :, 0:NK8]), R=[histr], W=[histbr])
                E(POOL, lambda: nc.gpsimd.tensor_copy(out=hist[:, :, :, 0], in_=hist[:, :, :, NK8]), R=[histr], W=[histr])
            else:
                E(DVE, lambda: nc.vector.tensor_copy(out=hist[:, :, :, 0], in_=hist[:, :, :, NK8]), R=[histr], W=[histr])
            if full:
                for b in range(8):
                    wc32, wc64, kb, cr = load_s5cc(b)
                    p_, p_r = nps()
                    pv = p_[:, 0:TT].rearrange("p (k j) -> p k j", j=8)
                    uv = uT[:, b, :].rearrange("p (k j) -> p k j", j=8)
                    for tau in range(8):
                        mm(pv[:, :, tau:8], kb[:, tau, :], uv[:, :, 0:8 - tau], tau == 0, False, R=[cr, uTr[b]], W=[p_r])
                    for r in range(4):
                        if r < 2:
                            pvr = p_[32 * r:32 * r + 32, 0:TT].rearrange("p (k j) -> p k j", j=8)
                        else:
                            pvr = p_[64:128, 0:TT].rearrange("p (k j) -> p k j", j=8)
                        for j in range(8):
                            for ri in range(2):
                                last = (r == 3 and j == 7 and ri == 1)
                                lhs = wc32[:, r, j, ri, :] if r < 2 else wc64[:, r - 2, j, ri, :]
                                mm(pvr[:, :, j], lhs, histb[:, ri, 4 * b + r, :], False, last, R=[cr, histbr], W=[p_r])
                    y_, y_r = ntmp()
                    stt(y_[:, 0:TT], uT[:, b, :], DqT[:, b:b + 1], p_[:, 0:TT], ALU.mult, ALU.add, R=[uTr[b], p_r] + C, W=[y_r])
                    act(zT[:, b, :], y_[:, 0:TT], AF.Gelu, R=[y_r], W=[zTr])
                for g4 in range(4):
                    wsb, wr = wload(w_glu[:, g4 * 256:(g4 + 1) * 256], 1024, 256)
                    for j in range(2):
                        m = g4 * 2 + j
                        p_, p_r = fm_group(wsb, wr, 8, j * 128, zT, [zTr])
                        sg, sgr = ntmp()
                        act(sg[:, 0:TT], p_[:, 0:TT], AF.Sigmoid, R=[p_r] + C, W=[sgr], bias=bgluT[:, m:m + 1])
                        tt(s5o[:, m, :], zT[:, m, :], sg[:, 0:TT], ALU.mult, R=[zTr, sgr], W=[s5or])

            if not full:
                if has_next:
                    norm_b(gmixT)
                continue
            for m2 in range(8):
                w1, w1r = wload(w_ps5[:, m2 * 256:(m2 + 1) * 256], 1024, 256)
                w2, w2r = wload(w_in[:, 5120 + m2 * 256:5120 + (m2 + 1) * 256], D, 256)
                w3, w3r = wload(w_phg[:, m2 * 256:(m2 + 1) * 256], 1024, 256)
                w4, w4r = wload(w_in[:, 7168 + m2 * 256:7168 + (m2 + 1) * 256], D, 256)
                for j in range(2):
                    m = m2 * 2 + j
                    p1, p1r = fm_group(w1, w1r, 8, j * 128, s5o, [s5or])
                    p2, p2r = fm_group(w2, w2r, 16, j * 128, hT, hTr)
                    g1, g1r = ntmp()
                    act(g1[:, 0:TT], p2[:, 0:TT], AF.Sigmoid, R=[p2r], W=[g1r])
                    tt(g1[:, 0:TT], g1[:, 0:TT], p1[:, 0:TT], ALU.mult, R=[g1r, p1r], W=[g1r])
                    p3, p3r = fm_group(w3, w3r, 8, j * 128, ohT, [ohTr])
                    p4, p4r = fm_group(w4, w4r, 16, j * 128, hT, hTr)
                    g2, g2r = ntmp()
                    act(g2[:, 0:TT], p4[:, 0:TT], AF.Sigmoid, R=[p4r], W=[g2r])
                    tt(g2[:, 0:TT], g2[:, 0:TT], p3[:, 0:TT], ALU.mult, R=[g2r, p3r], W=[g2r])
                    tt(mT[:, m, :], g1[:, 0:TT], g2[:, 0:TT], ALU.add, R=[g1r, g2r], W=mTr)
            if pend_epi is not None:
                epilogue(*pend_epi)
                pend_epi = None
            for n in range(8):
                wsb, wr = wload(w_out[:, n * 256:(n + 1) * 256], D, 256)
                for sub in range(NSUB):
                    p_, p_r = nps()
                    for kk in range(16):
                        mm(p_[:, 0:256], mT[:, kk, sub * 128:(sub + 1) * 128], wsb[:, kk, :], kk == 0, kk == 15, R=[wr] + mTr, W=[p_r])
                    tt(xres[:, xb, sub, n * 256:(n + 1) * 256], xres[:, xb, sub, n * 256:(n + 1) * 256], p_[:, 0:256], ALU.add, R=[p_r, xr[xb][sub]], W=[xr[xb][sub]])
            xload(t + 1)
            rmsnorm_to_hT(gffnT, xb)
            for c in range(NFC):
                if c % 2 == 0:
                    wg, wgr = wload(w_up[:, (c // 2) * 256:(c // 2 + 1) * 256], D, 256)
                    wv, wvr = wload(w_up[:, DFF + (c // 2) * 256:DFF + (c // 2 + 1) * 256], D, 256)
                pg, pgr = fm_group(wg, wgr, 16, (c % 2) * 128, hT, hTr)
                pv_, pvr_ = fm_group(wv, wvr, 16, (c % 2) * 128, hT, hTr)
                if not main:
                    act(chist[:, c, 0, :], pg[:, TT - 2:TT], AF.Copy, R=[pgr], W=[chr_[c]])
                    act(chist[:, c, 1, :], pv_[:, TT - 2:TT], AF.Copy, R=[pvr_], W=[chr_[c]])
                    continue
                ug, ugr = ntmp()
                uv_, uvr = ntmp()
                act(ug[:, 2:TT + 2], pg[:, 0:TT], AF.Copy, R=[pgr], W=[ugr])
                act(uv_[:, 2:TT + 2], pv_[:, 0:TT], AF.Copy, R=[pvr_], W=[uvr])
                act(ug[:, 0:2], chist[:, c, 0, :], AF.Copy, R=[chr_[c]], W=[ugr])
                act(uv_[:, 0:2], chist[:, c, 1, :], AF.Copy, R=[chr_[c]], W=[uvr])
                act(chist[:, c, 0, :], ug[:, TT:TT + 2], AF.Copy, R=[ugr], W=[chr_[c]])
                act(chist[:, c, 1, :], uv_[:, TT:TT + 2], AF.Copy, R=[uvr], W=[chr_[c]])
                res_ = []
                for vi, (u_, u_r) in enumerate(((ug, ugr), (uv_, uvr))):
                    a_, a_r = ntmp()
                    act(a_[:, 0:TT], u_[:, 2:TT + 2], AF.Identity, R=[u_r] + C, W=[a_r], scale=cw[:, c, vi, 2:3], bias=cb[:, c, vi:vi + 1])
                    stt(a_[:, 0:TT], u_[:, 1:TT + 1], cw[:, c, vi, 1:2], a_[:, 0:TT], ALU.mult, ALU.add, R=[u_r, a_r] + C, W=[a_r])
                    stt(a_[:, 0:TT], u_[:, 0:TT], cw[:, c, vi, 0:1], a_[:, 0:TT], ALU.mult, ALU.add, R=[u_r, a_r] + C, W=[a_r])
                    res_.append((a_, a_r))
                (ag, agr), (av, avr) = res_
                act(ug[:, 0:TT], ag[:, 0:TT], AF.Silu, R=[agr], W=[ugr])
                tt(actT[:, c, :], ug[:, 0:TT], av[:, 0:TT], ALU.mult, R=[ugr, avr], W=actTR)
            if not main:
                if has_next:
                    rmsnorm_to_hT(gmixT, (t + 1) % 2)
                continue
            for n in range(8):
                if n == 2 and has_next:
                    norm_a((t + 1) % 2)
                if n == 5 and has_next:
                    norm_b(gmixT)
                pss = [nps() for _ in range(NSUB)]
                kg = 0
                for grp in (16, 16, 12):
                    wsb, wr = wload(w_down[kg * 128:(kg + grp) * 128, n * 256:(n + 1) * 256], grp * 128, 256)
                    for sub in range(NSUB):
                        for kk in range(grp):
                            kgl = kg + kk
                            mm(pss[sub][0][:, 0:256], actT[:, kgl, sub * 128:(sub + 1) * 128], wsb[:, kk, :], kgl == 0, kgl == NFC - 1,
                               R=[wr] + actTR, W=[pss[sub][1]])
                    kg += grp
                for sub in range(NSUB):
                    tt(xres[:, xb, sub, n * 256:(n + 1) * 256], xres[:, xb, sub, n * 256:(n + 1) * 256], pss[sub][0][:, 0:256], ALU.add,
                       R=[pss[sub][1], xr[xb][sub]], W=[xr[xb][sub]])
            pend_epi = (t, xb)
        if pend_epi is not None:
            epilogue(*pend_epi)
        if osem.cnt:
            nc.sync.wait_ge(osem.h, osem.cnt)
        k.barrier([PE, ACT, DVE, SP, POOL])
    return nc


_NC = None


def kernel(**inputs):
    global _NC
    x = np.ascontiguousarray(inputs["x"], dtype=np.float32)
    B, L, _ = x.shape
    ident = np.eye(128, dtype=np.float32)
    idx = np.arange(128)
    maskbd = ((idx[:, None] // 64 == idx[None, :] // 64) & (idx[:, None] <= idx[None, :])).astype(np.float32)
    rmask = np.ones((128, TT), np.float32)
    rmask[:, ::64] = 0.0
    me2 = np.stack([((idx // 16) % 2 == e) for e in range(2)], axis=1).astype(np.float32)
    me1 = np.stack([(idx // 64 == e) for e in range(2)], axis=1).astype(np.float32)
    mkc = np.stack([(idx // 32 == r_) for r_ in range(4)], axis=1).astype(np.float32)
    shared = {
        "ln_mix_g": inputs["ln_mix_g"][0], "w_in": inputs["w_in"][0],
        "s5_a_re": inputs["s5_a_re"][0], "s5_a_im": inputs["s5_a_im"][0], "s5_log_dt": inputs["s5_log_dt"][0],
        "s5_b_re": inputs["s5_b_re"][0], "s5_b_im": inputs["s5_b_im"][0], "s5_c_re": inputs["s5_c_re"][0],
        "s5_c_im": inputs["s5_c_im"][0], "s5_d": inputs["s5_d"][0], "s5_w_glu": inputs["s5_w_glu"][0],
        "s5_b_glu": inputs["s5_b_glu"][0], "w_proj_s5": inputs["w_proj_s5"][0], "hgrn_lb_logits": inputs["hgrn_lb_logits"],
        "hgrn_norm_g": inputs["hgrn_norm_g"][0], "w_proj_hgrn": inputs["w_proj_hgrn"][0], "w_out": inputs["w_out"][0],
        "ln_ffn_g": inputs["ln_ffn_g"][0], "w_up": inputs["w_up"][0], "conv_w": inputs["conv_w"][0],
        "conv_b": inputs["conv_b"][0], "w_down": inputs["w_down"][0], "ln_final_g": inputs["ln_final_g"],
        "c_mk": mkc, "c_ident": ident, "c_identf": ident, "c_maskbd": maskbd, "c_rmask": rmask, "c_me2": me2, "c_me1": me1,
    }
    shared = {k_: np.ascontiguousarray(v, dtype=np.float32) for k_, v in shared.items()}
    in_maps = []
    for c in range(8):
        b, h = c // 2, c % 2
        xi = np.zeros((2 * HALF, D), np.float32)
        if h == 1:
            xi[:HALF] = x[b, :HALF]
        xi[HALF:] = x[b, h * HALF:(h + 1) * HALF]
        m = dict(shared)
        m["xin"] = xi
        in_maps.append(m)
    if _NC is None:
        _NC = build_nc()
    res = run_bass_kernel_spmd(_NC, in_maps, core_ids=list(range(8)))
    outp = np.empty((B, L, D), np.float32)
    for c in range(8):
        b, h = c // 2, c % 2
        outp[b, h * HALF:(h + 1) * HALF] = res.results[c]["out"]
    return outp
```

```python
import contextlib
import math
import numpy as np
import concourse.bass as bass
import concourse.mybir as mybir
from concourse.bass_utils import run_bass_kernel_spmd

F32 = mybir.dt.float32
BF16 = mybir.dt.bfloat16
I32 = mybir.dt.int32
AF = mybir.ActivationFunctionType
ALU = mybir.AluOpType

D = 2048
TT = 256
NSUB = TT // 128
NCH = TT // 64
NK8 = TT // 8
HALF = 2048
NWARM = HALF // TT
NTILE = 2 * NWARM
DFF = 5632
NFC = DFF // 128
EPS = 1e-6
NSLOT = 6
SLOT = 4096
NRING = 5
NTMP = 8


class Sem:
    def __init__(self, h):
        self.h = h
        self.cnt = 0


class Res:
    __slots__ = ("w", "r")

    def __init__(self):
        self.w = None
        self.r = {}


class Eng:
    def __init__(self, q, sem, is_pe=False):
        self.q = q
        self.sem = sem
        self.seen = {}
        self.is_pe = is_pe


class KB:
    def __init__(self, nc, es):
        self.nc = nc
        self.es = es
        self.nsem = 0
        self.PE = Eng(nc.tensor, self.sem("pe"), True)
        self.ACT = Eng(nc.scalar, self.sem("act"))
        self.DVE = Eng(nc.vector, self.sem("dve"))
        self.POOL = Eng(nc.gpsimd, self.sem("pool"))
        self.SP = Eng(nc.sync, self.sem("sp"))
        self.engs = [self.PE, self.ACT, self.DVE, self.POOL, self.SP]
        self.allsems = []

    def sem(self, name):
        self.nsem += 1
        s = Sem(self.es.enter_context(self.nc.semaphore(f"{name}_{self.nsem}")))
        return s

    def sb(self, name, shape, dt):
        return self.es.enter_context(self.nc.sbuf_tensor(name, shape, dt))

    def _waits(self, eng, R, W):
        deps = {}

        def add(tok):
            if tok is None:
                return
            s, v = tok
            if deps.get(s, 0) < v:
                deps[s] = v
        for r in R:
            add(r.w)
        for w in W:
            add(w.w)
            for s, v in w.r.items():
                add((s, v))
        for s, v in deps.items():
            if eng.is_pe and s is eng.sem:
                continue
            if eng.seen.get(s, 0) >= v:
                continue
            eng.q.wait_ge(s.h, v)
            eng.seen[s] = v

    def _commit(self, tok, R, W):
        s, v = tok
        for w in W:
            w.w = tok
            w.r = {}
        for r in R:
            if r.r.get(s, 0) < v:
                r.r[s] = v

    def emit(self, eng, fn, R=(), W=()):
        self._waits(eng, R, W)
        ins = fn()
        eng.sem.cnt += 1
        ins.then_inc(eng.sem.h, 1)
        tok = (eng.sem, eng.sem.cnt)
        self._commit(tok, R, W)
        return tok

    def dma(self, eng, fn, sem, R=(), W=()):
        self._waits(eng, R, W)
        ins = fn()
        sem.cnt += 16
        ins.then_inc(sem.h, 16)
        tok = (sem, sem.cnt)
        self._commit(tok, R, W)
        return tok

    def barrier(self, engs=None):
        engs = engs or [self.PE, self.ACT, self.DVE]
        for e in engs:
            for o in engs:
                if o is e or o.sem.cnt == 0:
                    continue
                if e.seen.get(o.sem, 0) < o.sem.cnt:
                    e.q.wait_ge(o.sem.h, o.sem.cnt)
                    e.seen[o.sem] = o.sem.cnt


def build_nc():
    nc = bass.Bass("TRN2", target_bir_lowering=False)

    def din(name, shape):
        return nc.dram_tensor(name, list(shape), F32, kind="ExternalInput").ap()
    xin = din("xin", [2 * HALF, D])
    ln_mix_g = din("ln_mix_g", [D])
    w_in = din("w_in", [D, 9216])
    a_re = din("s5_a_re", [64, 64])
    a_im = din("s5_a_im", [64, 64])
    log_dt = din("s5_log_dt", [64])
    b_re = din("s5_b_re", [64, 64, 16])
    b_im = din("s5_b_im", [64, 64, 16])
    c_re = din("s5_c_re", [64, 16, 64])
    c_im = din("s5_c_im", [64, 16, 64])
    s5_d = din("s5_d", [64, 16])
    w_glu = din("s5_w_glu", [1024, 1024])
    b_glu = din("s5_b_glu", [1024])
    w_ps5 = din("w_proj_s5", [1024, D])
    lb_logits = din("hgrn_lb_logits", [2, 1024])
    norm_g = din("hgrn_norm_g", [1024])
    w_phg = din("w_proj_hgrn", [1024, D])
    w_out = din("w_out", [D, D])
    ln_ffn_g = din("ln_ffn_g", [D])
    w_up = din("w_up", [D, 2 * DFF])
    conv_w = din("conv_w", [3, 2 * DFF])
    conv_b = din("conv_b", [2 * DFF])
    w_down = din("w_down", [DFF, D])
    ln_fin_g = din("ln_final_g", [D])
    c_ident = din("c_ident", [128, 128])
    c_identf = din("c_identf", [128, 128])
    c_maskbd = din("c_maskbd", [128, 128])
    c_rmask = din("c_rmask", [128, TT])
    c_me2 = din("c_me2", [128, 2])
    c_me1 = din("c_me1", [128, 2])
    c_mk = din("c_mk", [128, 4])
    out = nc.dram_tensor("out", [HALF, D], F32, kind="ExternalOutput").ap()
    s5d = nc.dram_tensor("s5d", [8, 128, 6144], BF16, kind="Internal").ap()
    wsc = nc.dram_tensor("wsc", [160, 128, SLOT], BF16, kind="Internal").ap()

    es = contextlib.ExitStack()
    with es:
        k = KB(nc, es)
        PE, ACT, DVE, POOL, SP = k.PE, k.ACT, k.DVE, k.POOL, k.SP
        E = k.emit

        identb = k.sb("identb", [128, 128], BF16)
        identf = k.sb("identf", [128, 128], F32)
        onesb = k.sb("onesb", [128, 128], BF16)
        maskbd = k.sb("maskbd", [128, 128], F32)
        rmask = k.sb("rmask", [128, TT], F32)
        gmixT = k.sb("gmixT", [128, 16], F32)
        gffnT = k.sb("gffnT", [128, 16], F32)
        gfinB = k.sb("gfinB", [128, D], F32)
        bgluT = k.sb("bgluT", [128, 8], F32)
        ngT = k.sb("ngT", [128, 8], F32)
        lbT = k.sb("lbT", [128, 8], F32)
        omlT = k.sb("omlT", [128, 8], F32)
        lgT = k.sb("lgT", [128, 2, 8], F32)
        DqT = k.sb("DqT", [128, 8], F32)
        cw = k.sb("cw", [128, NFC, 2, 3], F32)
        cb = k.sb("cb", [128, NFC, 2], F32)
        M1 = k.sb("M1", [128, 2, 32], F32)
        M2 = k.sb("M2", [128, 2, 32], F32)
        me2 = k.sb("me2", [128, 2], F32)
        me1 = k.sb("me1", [128, 2], F32)
        mk = k.sb("mk", [128, 4], F32)
        cres = Res()
        csem = k.sem("c")
        cres2 = Res()
        csem2 = k.sem("c2")

        ps = [es.enter_context(nc.psum_tensor(f"ps{i}", [128, 512], F32)) for i in range(6)]
        psr = [Res() for _ in range(6)]
        pb = [es.enter_context(nc.psum_tensor(f"pb{i}", [128, 1024], BF16)) for i in range(2)]
        pbr = [Res(), Res()]
        st = {"ps": 0, "pb": 0, "w": 0, "tmp": 0, "s5c": 0, "pb2": 0}

        def nps():
            i = st["ps"]
            st["ps"] = (i + 1) % NRING
            return ps[i], psr[i]

        def npb():
            i = st["pb"]
            st["pb"] = (i + 1) % 2
            return pb[i], pbr[i]

        def ntmp():
            i = st["tmp"]
            st["tmp"] = (i + 1) % NTMP
            return tmps[:, i, :], tmpr[i]

        greg = {}
        cvres = [Res() for _ in range(8)]
        cvsem = [k.sem("cv") for _ in range(8)]

        def gkey(src, K, ncols):
            return (src.tensor.name, int(src.offset), K, ncols)

        def convert(src, K, ncols, R=()):
            key = gkey(src, K, ncols)
            if key in greg:
                return
            gid = len(greg)
            r = Res()
            greg[key] = (gid, r)
            kc = K // 128
            assert kc * ncols <= SLOT
            j = gid % 8
            dst = wsc[gid, :, 0:kc * ncols].rearrange("p (k n) -> p k n", n=ncols)
            srcv = src.rearrange("(k p) n -> p k n", p=128)
            k.dma(POOL, lambda: nc.gpsimd.dma_start(out=dst, in_=srcv), cvsem[j], R=list(R), W=[r, cvres[j]])

        def wload(src, K, ncols):
            gid, gr = greg[gkey(src, K, ncols)]
            i = st["w"]
            st["w"] = (i + 1) % NSLOT
            kc = K // 128
            dst2 = wring[:, i, 0:kc * ncols]
            k.dma(SP, lambda: nc.sync.dma_start(out=dst2, in_=wsc[gid, :, 0:kc * ncols]), wsem[i], R=[gr], W=[wres[i]])
            return dst2.rearrange("p (k n) -> p k n", n=ncols), wres[i]

        def mm(out_, lhsT, rhs, start, stop, R, W):
            E(PE, lambda: nc.tensor.matmul(out_, lhsT=lhsT, rhs=rhs, start=start, stop=stop), R=R, W=W)

        def act(out_, in_, func, R, W, bias=None, scale=None, accum_out=None):
            kw = {}
            if bias is not None:
                kw["bias"] = bias
            if scale is not None:
                kw["scale"] = scale
            if accum_out is not None:
                kw["accum_out"] = accum_out
            E(ACT, lambda: nc.scalar.activation(out=out_, in_=in_, func=func, **kw), R=R, W=W)

        def tt(out_, in0, in1, op, R, W, eng=None):
            e = eng or DVE
            E(e, lambda: e.q.tensor_tensor(out=out_, in0=in0, in1=in1, op=op), R=R, W=W)

        def ts(out_, in0, s1, s2, op0, op1, R, W):
            if op1 is None:
                E(DVE, lambda: nc.vector.tensor_scalar(out=out_, in0=in0, scalar1=s1, scalar2=None, op0=op0), R=R, W=W)
            else:
                E(DVE, lambda: nc.vector.tensor_scalar(out=out_, in0=in0, scalar1=s1, scalar2=s2, op0=op0, op1=op1), R=R, W=W)

        def stt(out_, in0, scalar, in1, op0, op1, R, W):
            E(DVE, lambda: nc.vector.scalar_tensor_tensor(out=out_, in0=in0, scalar=scalar, in1=in1, op0=op0, op1=op1), R=R, W=W)

        def cdma2(eng, out_, in_, slow=True):
            q = eng.q
            k.dma(eng, lambda: q.dma_start(out=out_, in_=in_, allow_slow_non_contiguous=True), csem2, W=[cres2])

        def cdma(eng, out_, in_, slow=True):
            q = eng.q
            k.dma(eng, lambda: q.dma_start(out=out_, in_=in_, allow_slow_non_contiguous=True), csem, W=[cres])

        cdma(POOL, identb[:], c_ident)
        cdma(SP, identf[:], c_identf)
        cdma(SP, maskbd[:], c_maskbd)
        cdma(SP, rmask[:], c_rmask)
        cdma(SP, me2[:], c_me2)
        cdma(SP, me1[:], c_me1)
        cdma(SP, mk[:], c_mk)
        cdma(SP, gmixT[:], ln_mix_g.rearrange("(k p) -> p k", p=128), slow=True)
        cdma(SP, lgT[:], lb_logits.rearrange("l (k p) -> p l k", p=128), slow=True)

        ss = contextlib.ExitStack()
        with ss:
            def sbs(name, shape, dt=F32, stk=None):
                return (stk or ss).enter_context(nc.sbuf_tensor(name, shape, dt))
            s2 = contextlib.ExitStack()
            are1 = sbs("are1", [128, 32]); aim1 = sbs("aim1", [128, 32]); ldt1 = sbs("ldt1", [128, 32])
            B1 = sbs("B1", [128, 2, 32, 16]); C1 = sbs("C1", [128, 2, 32, 16])
            are2 = sbs("are2", [128, 8, 64], stk=s2); aim2 = sbs("aim2", [128, 8, 64], stk=s2); ldt2s = sbs("ldt2s", [128, 8], stk=s2)
            BT2 = sbs("BT2", [128, 2, 8, 64], stk=s2)
            BN = sbs("BN", [64, 2, 8, 8, 16], stk=s2)
            CL3 = sbs("CL3", [64, 2, 8, 2, 64], stk=s2)
            cdma(SP, are1[:], a_re.rearrange("(pr e) p -> (e p) pr", e=2), slow=True)
            cdma(SP, aim1[:], a_im.rearrange("(pr e) p -> (e p) pr", e=2), slow=True)
            for e in range(2):
                cdma(SP, ldt1[64 * e:64 * e + 64, :], log_dt.rearrange("(pr e) -> e pr", e=2)[e].partition_broadcast(64))
            for ri, (bsrc, csrc) in enumerate(((b_re, c_re), (b_im, c_im))):
                cdma(SP, B1[:, ri, :, :], bsrc.rearrange("(pr e) p c -> (e p) pr c", e=2))
                for r_ in range(4):
                    for e in range(2):
                        cdma(SP, CL3[16 * r_:16 * r_ + 16, ri, :, e, :], csrc.rearrange("(b r e) c p -> r e c b p", r=4, e=2)[r_, e])
                for q in range(8):
                    cdma(SP, BN[:, ri, :, q, :], bsrc.rearrange("(b q) p c -> q p b c", q=8)[q])
            for q in range(8):
                cdma(SP, are2[16 * q:16 * q + 16, :, :], a_re.rearrange("(b q) p -> q b p", q=8)[q].partition_broadcast(16))
                cdma(SP, aim2[16 * q:16 * q + 16, :, :], a_im.rearrange("(b q) p -> q b p", q=8)[q].partition_broadcast(16))
                cdma(SP, ldt2s[16 * q:16 * q + 16, :], log_dt.rearrange("(b q) -> q b", q=8)[q].partition_broadcast(16), slow=True)
            cres.w = (csem, csem.cnt)
            cdma2(SP, gffnT[:], ln_ffn_g.rearrange("(k p) -> p k", p=128), slow=True)
            cdma2(SP, gfinB[:], ln_fin_g.partition_broadcast(128))
            cdma2(SP, bgluT[:], b_glu.rearrange("(k p) -> p k", p=128), slow=True)
            cdma2(SP, ngT[:], norm_g.rearrange("(k p) -> p k", p=128), slow=True)
            cdma2(SP, DqT[:], s5_d.rearrange("(b q) c -> (q c) b", q=8), slow=True)
            for v_ in range(2):
                for j_ in range(3):
                    cdma2(SP, cw[:, :, v_, j_], conv_w[j_, v_ * DFF:(v_ + 1) * DFF].rearrange("(c p) -> p c", p=128), slow=True)
                cdma2(SP, cb[:, :, v_], conv_b[v_ * DFF:(v_ + 1) * DFF].rearrange("(c p) -> p c", p=128), slow=True)
            cres2.w = (csem2, csem2.cnt)

            C = [cres]
            for g4 in range(4):
                convert(w_in[:, g4 * 256:(g4 + 1) * 256], D, 256)
            for hd in range(8):
                convert(w_in[:, 2048 + hd * 128:2048 + (hd + 1) * 128], D, 128)
                convert(w_in[:, 3072 + hd * 128:3072 + (hd + 1) * 128], D, 128)
            for ri in range(2):
                pk, pkr = nps()
                for b in range(8):
                    E(PE, lambda: nc.tensor.transpose(out=pk[:, b * 64:(b + 1) * 64], in_=CL3[:, ri, b].rearrange("p e x -> p (e x)"),
                                                      identity=identf[0:64, 0:64]), R=C, W=[pkr])
                act(C1[:, ri].rearrange("p r c -> p (r c)"), pk[:, 0:512], AF.Copy, R=[pkr], W=C)
                pk, pkr = nps()
                for b in range(8):
                    E(PE, lambda: nc.tensor.transpose(out=pk[:, b * 64:(b + 1) * 64], in_=BN[:, ri, b].rearrange("p q c -> p (q c)"),
                                                      identity=identf[0:64, 0:64]), R=C, W=[pkr])
                act(BT2[:, ri].rearrange("p b x -> p (b x)"), pk[:, 0:512], AF.Copy, R=[pkr], W=C)
            E(DVE, lambda: nc.vector.memset(onesb[:], 1.0), W=C)
            tt(lbT[:], lgT[:, 1, :], lgT[:, 0, :], ALU.subtract, R=C, W=C)
            act(lbT[:], lbT[:], AF.Exp, R=C, W=C)
            ts(lbT[:], lbT[:], 1.0, None, ALU.add, None, R=C, W=C)
            E(DVE, lambda: nc.vector.reciprocal(out=lbT[:], in_=lbT[:]), R=C, W=C)
            ts(omlT[:], lbT[:], -1.0, 1.0, ALU.mult, ALU.add, R=C, W=C)

            def s5_scalars(tag, are, aim, ldt, shp, stk, npow):
                fs = [128] + shp
                nm = [0]
                tstk = contextlib.ExitStack()

                def P(dt=F32):
                    nm[0] += 1
                    return sbs(f"{tag}{nm[0]}", fs, dt, stk=stk)
                one = P(); zero = P(); cre = P(); cim = P()
                pws = [(P(), P()) for _ in range(npow)]

                def T(dt=F32):
                    nm[0] += 1
                    return sbs(f"{tag}{nm[0]}", fs, dt, stk=tstk)
                lre = T(); dt_ = T(); xr_ = T(); an = T()
                ts(lre[:], are, -1e-4, None, ALU.min, None, R=C, W=C)
                act(dt_[:], ldt, AF.Exp, R=C, W=C)
                tt(xr_[:], lre[:], dt_[:], ALU.mult, R=C, W=C)
                tt(an[:], aim, dt_[:], ALU.mult, R=C, W=C)
                pw = []
                E(DVE, lambda: nc.vector.memset(one[:], 1.0), W=C)
                E(DVE, lambda: nc.vector.memset(zero[:], 0.0), W=C)
                pw.append((one, zero))
                t_ = T(); ti = T(I32); tf = T(); r_ = T(); s1 = T(); s2 = T(); c2 = T(); mag = T()
                for j in range(1, npow + 1):
                    pre, pim = pws[j - 1]
                    act(mag[:], xr_[:], AF.Exp, R=C, W=C, scale=float(j))
                    ts(t_[:], an[:], float(j) / (2 * math.pi), None, ALU.mult, None, R=C, W=C)
                    E(DVE, lambda: nc.vector.tensor_copy(out=ti[:], in_=t_[:]), R=C, W=C)
                    E(DVE, lambda: nc.vector.tensor_copy(out=tf[:], in_=ti[:]), R=C, W=C)
                    tt(r_[:], t_[:], tf[:], ALU.subtract, R=C, W=C)
                    act(s2[:], r_[:], AF.Sin, R=C, W=C, scale=math.pi)
                    act(s1[:], r_[:], AF.Sin, R=C, W=C, scale=math.pi / 2)
                    tt(c2[:], s1[:], s1[:], ALU.mult, R=C, W=C)
                    ts(c2[:], c2[:], -2.0, 1.0, ALU.mult, ALU.add, R=C, W=C)
                    tt(pim[:], s2[:], c2[:], ALU.mult, R=C, W=C)
                    ts(pim[:], pim[:], 2.0, None, ALU.mult, None, R=C, W=C)
                    tt(pre[:], s2[:], s2[:], ALU.mult, R=C, W=C)
                    ts(pre[:], pre[:], -2.0, 1.0, ALU.mult, ALU.add, R=C, W=C)
                    tt(pre[:], pre[:], mag[:], ALU.mult, R=C, W=C)
                    tt(pim[:], pim[:], mag[:], ALU.mult, R=C, W=C)
                    pw.append((pre, pim))
                den = T(); nr = T(); t2 = T()
                tt(den[:], lre[:], lre[:], ALU.mult, R=C, W=C)
                tt(t2[:], aim, aim, ALU.mult, R=C, W=C)
                tt(den[:], den[:], t2[:], ALU.add, R=C, W=C)
                E(DVE, lambda: nc.vector.reciprocal(out=den[:], in_=den[:]), R=C, W=C)
                ts(nr[:], pw[1][0][:], -1.0, None, ALU.add, None, R=C, W=C)
                ni = pw[1][1]
                tt(cre[:], nr[:], lre[:], ALU.mult, R=C, W=C)
                tt(t2[:], ni[:], aim, ALU.mult, R=C, W=C)
                tt(cre[:], cre[:], t2[:], ALU.add, R=C, W=C)
                tt(cre[:], cre[:], den[:], ALU.mult, R=C, W=C)
                tt(cim[:], ni[:], lre[:], ALU.mult, R=C, W=C)
                tt(t2[:], nr[:], aim, ALU.mult, R=C, W=C)
                tt(cim[:], cim[:], t2[:], ALU.subtract, R=C, W=C)
                tt(cim[:], cim[:], den[:], ALU.mult, R=C, W=C)
                tstk.close()
                return pw, cre, cim

            pw2, cre2, cim2 = s5_scalars("b", are2[:], aim2[:], ldt2s[:].rearrange("p (b o) -> p b o", o=1).broadcast_to([128, 8, 64]), [8, 64], s2, 7)

            WA = sbs("WA", [128, 8, 8, 2, 2, 64], BF16, stk=s2)
            Bc2r = sbs("Bc2r", [128, 8, 64], stk=s2); Bc2i = sbs("Bc2i", [128, 8, 64], stk=s2); u1 = sbs("u1", [128, 8, 64], stk=s2); u2 = sbs("u2", [128, 8, 64], stk=s2)
            tt(Bc2r[:], cre2[:], BT2[:, 0], ALU.mult, R=C, W=C)
            tt(u1[:], cim2[:], BT2[:, 1], ALU.mult, R=C, W=C)
            tt(Bc2r[:], Bc2r[:], u1[:], ALU.subtract, R=C, W=C)
            tt(Bc2i[:], cre2[:], BT2[:, 1], ALU.mult, R=C, W=C)
            tt(u1[:], cim2[:], BT2[:, 0], ALU.mult, R=C, W=C)
            tt(Bc2i[:], Bc2i[:], u1[:], ALU.add, R=C, W=C)
            for i in range(8):
                Ar, Ai = pw2[7 - i]
                tt(u1[:], Ar[:], Bc2r[:], ALU.mult, R=C, W=C)
                tt(u2[:], Ai[:], Bc2i[:], ALU.mult, R=C, W=C)
                tt(u1[:], u1[:], u2[:], ALU.subtract, R=C, W=C)
                for e in range(2):
                    ts(WA[:, :, i, 0, e, :], u1[:], me2[:, e:e + 1], None, ALU.mult, None, R=C, W=C)
                tt(u1[:], Ar[:], Bc2i[:], ALU.mult, R=C, W=C)
                tt(u2[:], Ai[:], Bc2r[:], ALU.mult, R=C, W=C)
                tt(u1[:], u1[:], u2[:], ALU.add, R=C, W=C)
                for e in range(2):
                    ts(WA[:, :, i, 1, e, :], u1[:], me2[:, e:e + 1], None, ALU.mult, None, R=C, W=C)

            for b in range(8):
                k.dma(SP, lambda: nc.sync.dma_start(out=s5d[b, :, 0:2048], in_=WA[:, b].rearrange("p i r e q -> p (i r e q)")), csem, R=C, W=C)
            for e_ in (PE, ACT, DVE, SP, POOL):
                k._waits(e_, [cres], [cres])
            k.barrier([PE, ACT, DVE, SP, POOL])
            s2.close()
            pw1, cre1, cim1 = s5_scalars("a", are1[:], aim1[:], ldt1[:], [32], ss, 8)
            E(DVE, lambda: nc.vector.tensor_copy(out=M1[:, 0, :], in_=pw1[8][0][:]), R=C, W=C)
            E(DVE, lambda: nc.vector.tensor_copy(out=M1[:, 1, :], in_=pw1[8][0][:]), R=C, W=C)
            ts(M2[:, 0, :], pw1[8][1][:], -1.0, None, ALU.mult, None, R=C, W=C)
            E(DVE, lambda: nc.vector.tensor_copy(out=M2[:, 1, :], in_=pw1[8][1][:]), R=C, W=C)

            WC = sbs("WC", [128, 32, 9, 2, 2, 16], BF16)
            BP = sbs("BP", [128, 32, 2, 2, 16], BF16)
            v1 = sbs("v1", [128, 32, 16]); v2 = sbs("v2", [128, 32, 16]); Bc1r = sbs("Bc1r", [128, 32, 16]); Bc1i = sbs("Bc1i", [128, 32, 16])

            def bc16(t):
                return t[:].rearrange("p (f o) -> p f o", o=1).broadcast_to([128, 32, 16])
            for m in range(9):
                Ar, Ai = pw1[m]
                tt(v1[:], C1[:, 0], bc16(Ar), ALU.mult, R=C, W=C)
                tt(v2[:], C1[:, 1], bc16(Ai), ALU.mult, R=C, W=C)
                tt(v1[:], v1[:], v2[:], ALU.subtract, R=C, W=C)
                for e in range(2):
                    ts(WC[:, :, m, 0, e, :], v1[:], me1[:, e:e + 1], None, ALU.mult, None, R=C, W=C)
                tt(v1[:], C1[:, 0], bc16(Ai), ALU.mult, R=C, W=C)
                tt(v2[:], C1[:, 1], bc16(Ar), ALU.mult, R=C, W=C)
                tt(v1[:], v1[:], v2[:], ALU.add, R=C, W=C)
                ts(v1[:], v1[:], -1.0, None, ALU.mult, None, R=C, W=C)
                for e in range(2):
                    ts(WC[:, :, m, 1, e, :], v1[:], me1[:, e:e + 1], None, ALU.mult, None, R=C, W=C)
            tt(Bc1r[:], bc16(cre1), B1[:, 0], ALU.mult, R=C, W=C)
            tt(v1[:], bc16(cim1), B1[:, 1], ALU.mult, R=C, W=C)
            tt(Bc1r[:], Bc1r[:], v1[:], ALU.subtract, R=C, W=C)
            tt(Bc1i[:], bc16(cre1), B1[:, 1], ALU.mult, R=C, W=C)
            tt(v1[:], bc16(cim1), B1[:, 0], ALU.mult, R=C, W=C)
            tt(Bc1i[:], Bc1i[:], v1[:], ALU.add, R=C, W=C)
            for e in range(2):
                ts(BP[:, :, 0, e, :], Bc1r[:], me1[:, e:e + 1], None, ALU.mult, None, R=C, W=C)
                ts(BP[:, :, 1, e, :], Bc1i[:], me1[:, e:e + 1], None, ALU.mult, None, R=C, W=C)

            KBD = sbs("KBD", [128, 8, 8, 128], BF16)
            E(DVE, lambda: nc.vector.memset(KBD[:], 0.0), W=C)
            BP64 = sbs("BP64", [128, 32, 2, 2, 32], BF16)
            E(DVE, lambda: nc.vector.memset(BP64[:], 0.0), W=C)
            BPv = BP[:].rearrange("p (b r) i e c -> p b r i (e c)", r=4)
            BP64v = BP64[:].rearrange("p (b r) i s c -> p b r i s c", r=4)
            for r in (2, 3):
                E(DVE, lambda: nc.vector.tensor_copy(out=BP64v[:, :, r, :, r - 2, :], in_=BPv[:, :, r, :, :]), R=C, W=C)
            for b in range(8):
                pk, pkr = nps()
                for r in range(4):
                    pr = 4 * b + r
                    if r < 2:
                        o_ = pk[32 * r:32 * r + 32, 0:256].rearrange("p (t c) -> p t c", c=32)
                    else:
                        o_ = pk[64:128, (r - 2) * 256:(r - 1) * 256].rearrange("p (t c) -> p t c", c=32)
                    for ri in range(2):
                        if r < 2:
                            lhs = BP[:, pr, ri, :, :].rearrange("p e c -> p (e c)")
                        else:
                            lhs = BP64[:, pr, ri, :, :].rearrange("p s c -> p (s c)")
                        rhs = WC[:, pr, 0:8, ri, :, :].rearrange("p t e c -> p t (e c)")
                        mm(o_, lhs, rhs, ri == 0, ri == 1, R=C, W=[pkr])
                for r in range(4):
                    if r < 2:
                        o_ = pk[32 * r:32 * r + 32, 0:256].rearrange("p (t c) -> p t c", c=32)
                        act(KBD[32 * r:32 * r + 32, b, :, 32 * r:32 * r + 32], o_, AF.Copy, R=[pkr], W=C)
                    else:
                        o_ = pk[64:128, (r - 2) * 256:(r - 1) * 256].rearrange("p (t c) -> p t c", c=32)
                        act(KBD[64:128, b, :, 32 * r:32 * r + 32], o_, AF.Copy, R=[pkr], W=C)
            WC64 = sbs("WC64", [128, 8, 4, 8, 2, 64], BF16)
            scr = Res()
            E(DVE, lambda: nc.vector.memset(WC64[:, :, 2, :, :, 32:64], 0.0), R=C, W=C)
            E(DVE, lambda: nc.vector.memset(WC64[:, :, 3, :, :, 0:32], 0.0), R=C, W=C)
            WCv = WC[:].rearrange("p (b r) m i e c -> p b r m i (e c)", r=4)
            for r in range(4):
                c0 = 32 if r == 3 else 0
                for ri in range(2):
                    E(DVE, lambda: nc.vector.tensor_copy(out=WC64[:, :, r, :, ri, c0:c0 + 32], in_=WCv[:, :, r, 1:9, ri, :]), R=C, W=C)
            for b in range(8):
                k.dma(SP, lambda: nc.sync.dma_start(out=s5d[b, :, 2048:3072].rearrange("p (x c) -> p x c", c=32),
                                                    in_=WC64[:, b, 0:2, :, :, 0:32].rearrange("p r j i c -> p (r j i) c")), csem, R=C, W=[scr])
                k.dma(SP, lambda: nc.sync.dma_start(out=s5d[b, :, 3072:5120], in_=WC64[:, b, 2:4].rearrange("p r j i c -> p (r j i c)")), csem, R=C, W=[scr])
                k.dma(SP, lambda: nc.sync.dma_start(out=s5d[b, :, 5120:6144], in_=KBD[:, b].rearrange("p t c -> p (t c)")), csem, R=C, W=[scr])
            cres.w = (csem, csem.cnt)
            for e_ in (PE, ACT, DVE, SP, POOL):
                k._waits(e_, [cres], [cres])
            k.barrier([PE, ACT, DVE, SP, POOL])
        C = [cres, cres2]
        xres = k.sb("xres", [128, 2, NSUB, D], F32)
        xr = [[Res() for _ in range(NSUB)] for _ in range(2)]
        xsem = [[k.sem("x") for _ in range(NSUB)] for _ in range(2)]
        hT = k.sb("hT", [128, 16, TT], BF16)
        hTr = [Res() for _ in range(NSUB)]
        wring = k.sb("wring", [128, NSLOT, SLOT], BF16)
        wres = [Res() for _ in range(NSLOT)]
        wsem = [k.sem("w") for _ in range(NSLOT)]
        s5a = k.sb("s5a", [128, 2, 2048], BF16)
        s5ar = [Res(), Res()]
        s5asem = [k.sem("s5a"), k.sem("s5a")]
        s5cc = k.sb("s5cc", [128, 2, 4096], BF16)
        s5ccr = [Res(), Res()]
        s5ccsem = [k.sem("s5cc"), k.sem("s5cc")]
        uTm = k.sb("uTm", [128, 4, 8, TT], BF16)
        uTr = [Res() for _ in range(8)]
        mT = uTm[:, 0:2].rearrange("p a b t -> p (a b) t")
        mTr = uTr
        hist = k.sb("hist", [128, 2, 32, NK8 + 1], F32)
        histr = Res()
        arena = k.sb("arena", [128, NFC * TT], BF16)
        actT = arena[:, :].rearrange("p (c t) -> p c t", t=TT)
        uT = arena[:, 0:8 * TT].rearrange("p (b t) -> p b t", t=TT)
        histb = arena[:, 8 * TT:8 * TT + 64 * NK8].rearrange("p (i r k) -> p i r k", i=2, r=32)
        o1 = 8 * TT + 64 * NK8
        zT = arena[:, o1:o1 + 8 * TT].rearrange("p (b t) -> p b t", t=TT)
        s5o = arena[:, o1 + 8 * TT:o1 + 16 * TT].rearrange("p (b t) -> p b t", t=TT)
        ohT = arena[:, o1 + 16 * TT:o1 + 24 * TT].rearrange("p (b t) -> p b t", t=TT)
        assert o1 + 24 * TT <= NFC * TT
        histbr = Res(); zTr = Res(); s5or = Res(); ohTr = Res(); actTr = Res()
        actTR = [actTr] + uTr + [histbr, zTr, s5or, ohTr]
        tmps = k.sb("tmps", [128, NTMP, TT + 2], F32)
        tmpr = [Res() for _ in range(NTMP)]
        xn = k.sb("xn", [128, NSUB, D], BF16)
        xnr = [Res() for _ in range(NSUB)]
        ot = k.sb("ot", [128, D // 2], F32)
        otr = Res()
        osem = k.sem("o")
        S32 = k.sb("S32", [128, 8, 128], F32)
        Sb = k.sb("Sb", [128, 8, 128], BF16)
        Sr = [Res() for _ in range(8)]
        Sbr = [Res() for _ in range(8)]
        qdT = k.sb("qdT", [128, 2, TT], BF16)
        qdTr = [Res(), Res()]
        kdT = k.sb("kdT", [128, 2, TT], BF16)
        kdTr = [Res(), Res()]
        kdtok = k.sb("kdtok", [128, 2, 2, NSUB, 128], BF16)
        kdtokr = [Res(), Res()]
        vtok = k.sb("vtok", [128, 2, NSUB, 128], BF16)
        vtokr = [Res(), Res()]
        Efb = k.sb("Efb", [128, 2, TT], F32)
        Efr = [Res(), Res()]
        sggb = k.sb("sggb", [128, 2, TT], BF16)
        sggr = [Res(), Res()]
        scT = k.sb("scT", [128, 128], BF16)
        scTr = Res()
        sqo = k.sb("sqo", [128, 128], BF16)
        sqor = Res()
        chist = k.sb("chist", [128, NFC, 2, 2], F32)
        chr_ = [Res() for _ in range(NFC)]
        stat = k.sb("stat", [128, 8], F32)
        sbt = k.sb("sbt", [128, 2, 2, 2, 32], F32)
        sbtr = [[Res(), Res()], [Res(), Res()]]
        statr = Res()
        E(DVE, lambda: nc.vector.memset(S32[:], 0.0), W=Sr)
        E(DVE, lambda: nc.vector.memset(Sb[:], 0.0), W=Sbr)
        E(DVE, lambda: nc.vector.memset(hist[:], 0.0), W=[histr])
        E(DVE, lambda: nc.vector.memset(chist[:], 0.0), W=chr_)

        def load_s5a(b):
            i = st["s5c"]
            st["s5c"] = 1 - i
            k.dma(SP, lambda: nc.sync.dma_start(out=s5a[:, i, :], in_=s5d[b, :, 0:2048]), s5asem[i], R=C, W=[s5ar[i]])
            wa = s5a[:, i, :].rearrange("p (i r c) -> p i r c", i=8, r=2)
            return wa, s5ar[i]

        def load_s5cc(b):
            i = st["pb2"]
            st["pb2"] = 1 - i
            k.dma(SP, lambda: nc.sync.dma_start(out=s5cc[:, i, :], in_=s5d[b, :, 2048:6144]), s5ccsem[i], R=C, W=[s5ccr[i]])
            v = s5cc[:, i, :]
            wc32 = v[:, 0:1024].rearrange("p (r j i c) -> p r j i c", r=2, j=8, i=2)
            wc64 = v[:, 1024:3072].rearrange("p (r j i c) -> p r j i c", r=2, j=8, i=2)
            kb = v[:, 3072:4096].rearrange("p (t c) -> p t c", t=8)
            return wc32, wc64, kb, s5ccr[i]

        def norm_a(xb):
            for sub in range(NSUB):
                act(xn[:, sub, :], xres[:, xb, sub, :], AF.Square, R=[xr[xb][sub]], W=[xnr[sub], statr], accum_out=stat[:, 0:1])
                act(stat[:, 1:2], stat[:, 0:1], AF.Ln, R=[statr], W=[statr], scale=1.0 / D, bias=EPS)
                act(stat[:, 2:3], stat[:, 1:2], AF.Exp, R=[statr], W=[statr], scale=-0.5)
                ts(xn[:, sub, :], xres[:, xb, sub, :], stat[:, 2:3], None, ALU.mult, None, R=[xr[xb][sub], statr], W=[xnr[sub]])

        def norm_b(gT):
            for sub in range(NSUB):
                for h8 in range(2):
                    p_, p_r = npb()
                    for kk in range(8):
                        kc = h8 * 8 + kk
                        E(PE, lambda: nc.tensor.transpose(out=p_[:, kk * 128:(kk + 1) * 128], in_=xn[:, sub, kc * 128:(kc + 1) * 128], identity=identb[:]),
                          R=[xnr[sub]] + C, W=[p_r])
                    tt(hT[:, h8 * 8:h8 * 8 + 8, sub * 128:(sub + 1) * 128], p_[:].rearrange("p (k t) -> p k t", t=128),
                       gT[:, h8 * 8:h8 * 8 + 8].rearrange("p (k o) -> p k o", o=1).broadcast_to([128, 8, 128]), ALU.mult,
                       R=[p_r] + C, W=[hTr[sub]])

        def rmsnorm_to_hT(gT, xb):
            norm_a(xb)
            norm_b(gT)

        def fm_group(wsb, wr, kc_n, col0, actT_, actR):
            p_, p_r = nps()
            for kk in range(kc_n):
                mm(p_[:, 0:TT], wsb[:, kk, col0:col0 + 128], actT_[:, kk, :], kk == 0, kk == kc_n - 1, R=[wr] + actR, W=[p_r])
            return p_, p_r

        late_convs = []
        for g4 in range(4):
            late_convs.append((w_glu[:, g4 * 256:(g4 + 1) * 256], 1024, 256))
        for hd in range(8):
            late_convs.append((w_in[:, 1024 + hd * 128:1024 + (hd + 1) * 128], D, 128))
            late_convs.append((w_in[:, 4096 + hd * 128:4096 + (hd + 1) * 128], D, 128))
        for m2 in range(8):
            late_convs.append((w_ps5[:, m2 * 256:(m2 + 1) * 256], 1024, 256))
            late_convs.append((w_in[:, 5120 + m2 * 256:5120 + (m2 + 1) * 256], D, 256))
            late_convs.append((w_phg[:, m2 * 256:(m2 + 1) * 256], 1024, 256))
            late_convs.append((w_in[:, 7168 + m2 * 256:7168 + (m2 + 1) * 256], D, 256))
        for n in range(8):
            late_convs.append((w_out[:, n * 256:(n + 1) * 256], D, 256))
        for c2 in range(NFC // 2):
            late_convs.append((w_up[:, c2 * 256:(c2 + 1) * 256], D, 256))
            late_convs.append((w_up[:, DFF + c2 * 256:DFF + (c2 + 1) * 256], D, 256))
        for n in range(8):
            kg = 0
            for grp in (16, 16, 12):
                late_convs.append((w_down[kg * 128:(kg + grp) * 128, n * 256:(n + 1) * 256], grp * 128, 256))
                kg += grp


        def xload(tn):
            if tn >= NTILE:
                return
            for sub in range(NSUB):
                r0 = tn * TT + sub * 128
                k.dma(ACT, lambda: nc.scalar.dma_start(out=xres[:, tn % 2, sub, :], in_=xin[r0:r0 + 128, :]), xsem[tn % 2][sub], W=[xr[tn % 2][sub]])

        def epilogue(te, xe):
            for sub in range(NSUB):
                act(xn[:, sub, :], xres[:, xe, sub, :], AF.Square, R=[xr[xe][sub]], W=[xnr[sub], statr], accum_out=stat[:, 3:4])
                act(stat[:, 4:5], stat[:, 3:4], AF.Ln, R=[statr], W=[statr], scale=1.0 / D, bias=EPS)
                act(stat[:, 5:6], stat[:, 4:5], AF.Exp, R=[statr], W=[statr], scale=-0.5)
                r0 = (te - NWARM) * TT + sub * 128
                for hh in range(2):
                    hs = slice(hh * (D // 2), (hh + 1) * (D // 2))
                    ts(ot[:], xres[:, xe, sub, hs], stat[:, 5:6], None, ALU.mult, None, R=[xr[xe][sub], statr], W=[otr])
                    tt(ot[:], ot[:], gfinB[:, hs], ALU.mult, R=[otr] + C, W=[otr])
                    k.dma(ACT, lambda: nc.scalar.dma_start(out=out[r0:r0 + 128, hs], in_=ot[:]), osem, R=[otr], W=[otr])

        xload(0)
        for ci, cv in enumerate(late_convs):
            convert(*cv, R=(xr[0] if ci == 0 else ()))
        rmsnorm_to_hT(gmixT, 0)
        pend_epi = None
        for t in range(NTILE):
            full = t >= NWARM - 1
            main = t >= NWARM
            xb = t % 2
            has_next = t + 1 < NTILE
            if not full:
                xload(t + 1)

            for g4 in range(4):
                wsb, wr = wload(w_in[:, g4 * 256:(g4 + 1) * 256], D, 256)
                for j in range(2):
                    p_, p_r = fm_group(wsb, wr, 16, j * 128, hT, hTr)
                    act(uT[:, g4 * 2 + j, :], p_[:, 0:TT], AF.Copy, R=[p_r], W=[uTr[g4 * 2 + j]])
                    for v_ in range(4):
                        if v_ % 2 == 0:
                            ts(uTm[:, v_, g4 * 2 + j, :], p_[:, 0:TT], mk[:, v_:v_ + 1], None, ALU.mult, None, R=[p_r] + C, W=[uTr[g4 * 2 + j]])
                        else:
                            act(uTm[:, v_, g4 * 2 + j, :], p_[:, 0:TT], AF.Identity, R=[p_r] + C, W=[uTr[g4 * 2 + j]], scale=mk[:, v_:v_ + 1])
            for b in range(8):
                wa, cr = load_s5a(b)
                p_, p_r = nps()
                for r in range(4):
                    for ri in range(2):
                        o_ = p_[:, (r * 2 + ri) * NK8:(r * 2 + ri + 1) * NK8]
                        for i in range(8):
                            rhs = uTm[:, r, b, :].rearrange("p (k j) -> p k j", j=8)[:, :, i]
                            mm(o_, wa[:, i, ri, :], rhs, i == 0, i == 7, R=[cr, uTr[b]], W=[p_r])
                for ri in range(2):
                    act(hist[:, ri, 4 * b:4 * b + 4, 1:NK8 + 1],
                        p_[:, 0:8 * NK8].rearrange("p (r i k) -> p i r k", r=4, i=2)[:, ri], AF.Copy, R=[p_r], W=[histr])
            def stageB(k0, k1):
                sbe = POOL if full else DVE
                for kk in range(k0, k1):
                    t1v, t1r = sbt[:, kk % 2, 0], sbtr[kk % 2][0]
                    t2v, t2r = sbt[:, kk % 2, 1], sbtr[kk % 2][1]
                    tt(t1v, M1[:], hist[:, :, :, kk], ALU.mult, R=[histr] + C, W=[t1r], eng=sbe)
                    tt(t2v[:, 0, :], M2[:, 0, :], hist[:, 1, :, kk], ALU.mult, R=[histr] + C, W=[t2r], eng=sbe)
                    tt(t2v[:, 1, :], M2[:, 1, :], hist[:, 0, :, kk], ALU.mult, R=[histr] + C, W=[t2r], eng=sbe)
                    tt(t1v, t1v, t2v, ALU.add, R=[t1r, t2r], W=[t1r], eng=sbe)
                    tt(hist[:, :, :, kk + 1], hist[:, :, :, kk + 1], t1v, ALU.add, R=[t1r, histr], W=[histr], eng=sbe)

            def hgrn_s1_gen(hd):
                par = hd % 2
                wv_, wvr_ = wload(w_in[:, 3072 + hd * 128:3072 + (hd + 1) * 128], D, 128)

                def vblock(sub):
                    p_, p_r = nps()
                    for kk in range(16):
                        mm(p_[:, 0:128], hT[:, kk, sub * 128:(sub + 1) * 128], wv_[:, kk, :], kk == 0, kk == 15, R=[wvr_] + hTr, W=[p_r])
                    act(vtok[:, par, sub, :], p_[:, 0:128], AF.Copy, R=[p_r], W=[vtokr[par]])
                vblock(0)
                yield
                wsb, wr = wload(w_in[:, 2048 + hd * 128:2048 + (hd + 1) * 128], D, 128)
                p_, p_r = fm_group(wsb, wr, 16, 0, hT, hTr)
                f_, f_r = ntmp()
                act(f_[:, 0:TT], p_[:, 0:TT], AF.Sigmoid, R=[p_r], W=[f_r])
                ts(f_[:, 0:TT], f_[:, 0:TT], omlT[:, hd:hd + 1], lbT[:, hd:hd + 1], ALU.mult, ALU.add, R=[f_r] + C, W=[f_r])
                lf, lfr = ntmp()
                act(lf[:, 0:TT], f_[:, 0:TT], AF.Ln, R=[f_r], W=[lfr])
                bc, bcr = ntmp()
                E(DVE, lambda: nc.vector.tensor_tensor_scan(out=bc[:, 0:TT], data0=rmask[:], data1=lf[:, 0:TT], initial=0.0,
                                                            op0=ALU.mult, op1=ALU.add), R=[lfr] + C, W=[bcr])
                act(Efb[:, par, :], bc[:, 0:TT], AF.Exp, R=[bcr], W=[Efr[par]])
                Ei, Eir = ntmp()
                act(Ei[:, 0:TT], bc[:, 0:TT], AF.Exp, R=[bcr], W=[Eir], scale=-1.0)
                ts(f_[:, 0:TT], f_[:, 0:TT], -1.0, 1.0, ALU.mult, ALU.add, R=[f_r], W=[f_r])
                tt(kdT[:, par, :], f_[:, 0:TT], Ei[:, 0:TT], ALU.mult, R=[f_r, Eir], W=[kdTr[par]])
                yield
                for sub in range(1, NSUB):
                    vblock(sub)
                yield
                if full:
                    wsb, wr = wload(w_in[:, 1024 + hd * 128:1024 + (hd + 1) * 128], D, 128)
                    p_, p_r = fm_group(wsb, wr, 16, 0, hT, hTr)
                    sq, sqr = ntmp()
                    act(sq[:, 0:TT], p_[:, 0:TT], AF.Silu, R=[p_r], W=[sqr])
                    tt(qdT[:, par, :], sq[:, 0:TT], Efb[:, par, :], ALU.mult, R=[sqr, Efr[par]], W=[qdTr[par]])
                    yield
                    wsb, wr = wload(w_in[:, 4096 + hd * 128:4096 + (hd + 1) * 128], D, 128)
                    p_, p_r = fm_group(wsb, wr, 16, 0, hT, hTr)
                    act(sggb[:, par, :], p_[:, 0:TT], AF.Silu, R=[p_r], W=[sggr[par]])
                    yield

            def hgrn_s2a(hd):
                par = hd % 2
                pb_, pb_r = npb()
                for sub in range(NSUB):
                    E(PE, lambda: nc.tensor.transpose(out=pb_[:, sub * 128:(sub + 1) * 128], in_=kdT[:, par, sub * 128:(sub + 1) * 128], identity=identb[:]),
                      R=[kdTr[par]] + C, W=[pb_r])
                for hf_ in range(2):
                    act(kdtok[:, par, hf_], pb_[:, 0:NSUB * 128].rearrange("p (s c) -> p s c", c=128), AF.Identity,
                        R=[pb_r] + C, W=[kdtokr[par]], scale=me1[:, hf_:hf_ + 1])

            def hgrn_s2b_gen(hd):
                par = hd % 2
                po, por = ps[5], psr[5]
                for sub in range(NSUB):
                    tok = slice(sub * 128, (sub + 1) * 128)
                    if full:
                        psc, pscr = nps()
                        mm(psc[:, 0:128], kdT[:, par, tok], qdT[:, par, tok], True, True, R=[kdTr[par], qdTr[par]], W=[pscr])
                        tt(scT[:], psc[:, 0:128], maskbd[:], ALU.mult, R=[pscr] + C, W=[scTr])
                        yield
                        mm(po[:, 0:128], vtok[:, par, sub, :], scT[:], True, False, R=[vtokr[par], scTr], W=[por])
                    for hf in range(2):
                        c0 = sub * 128 + hf * 64
                        if full:
                            mm(po[:, hf * 64:hf * 64 + 64], Sb[:, hd, :], qdT[:, par, c0:c0 + 64], False, hf == 1, R=[Sbr[hd], qdTr[par]], W=[por])
                        ebl = Efb[:, par, c0 + 63:c0 + 64]
                        ts(S32[:, hd, :], S32[:, hd, :], ebl, None, ALU.mult, None, R=[Sr[hd], Efr[par]], W=[Sr[hd]])
                        pS, pSr = nps()
                        mm(pS[:, 0:128], kdtok[:, par, hf, sub, :], vtok[:, par, sub, :], True, True, R=[kdtokr[par], vtokr[par]], W=[pSr])
                        if full or t == NWARM - 2:
                            stt(Sb[:, hd, :], pS[:, 0:128], ebl, S32[:, hd, :], ALU.mult, ALU.add, R=[pSr, Sr[hd], Efr[par]], W=[Sbr[hd]])
                        stt(S32[:, hd, :], pS[:, 0:128], ebl, S32[:, hd, :], ALU.mult, ALU.add, R=[pSr, Sr[hd], Efr[par]], W=[Sr[hd]])
                        if full:
                            if hf == 1:
                                act(sqo[:], po[:, 0:128], AF.Square, R=[por], W=[sqor])
                            yield
                    if full:
                        pn, pnr = nps()
                        mm(pn[:, 0:128], onesb[:], sqo[:], True, True, R=[sqor] + C, W=[pnr])
                        rs, rsr = ntmp()
                        act(rs[:, 0:128], pn[:, 0:128], AF.Ln, R=[pnr], W=[rsr], scale=1.0 / 128, bias=EPS)
                        act(rs[:, 0:128], rs[:, 0:128], AF.Exp, R=[rsr], W=[rsr], scale=-0.5)
                        tt(rs[:, 0:128], po[:, 0:128], rs[:, 0:128], ALU.mult, R=[por, rsr], W=[rsr])
                        act(rs[:, 0:128], rs[:, 0:128], AF.Identity, R=[rsr] + C, W=[rsr], scale=ngT[:, hd:hd + 1])
                        tt(ohT[:, hd, tok], rs[:, 0:128], sggb[:, par, tok], ALU.mult, R=[rsr, sggr[par]], W=[ohTr])

            for _ in hgrn_s1_gen(0):
                pass
            for hd in range(8):
                if hd == 4 and not full and has_next:
                    norm_a((t + 1) % 2)
                hgrn_s2a(hd)
                gfill = hgrn_s1_gen(hd + 1) if hd + 1 < 8 else iter(())
                if not full:
                    for _ in gfill:
                        pass
                for _ in hgrn_s2b_gen(hd):
                    next(gfill, None)
                for _ in gfill:
                    pass
                stageB(hd * NK8 // 8, (hd + 1) * NK8 // 8)
            if full:
                E(POOL, lambda: nc.gpsimd.tensor_copy(out=histb, in_=hist[:, :, :, 0:NK8]), R=[histr], W=[histbr])
                E(POOL, lambda: nc.gpsimd.tensor_copy(out=hist[:, :, :, 0], in_=hist[:, :, :, NK8]), R=[histr], W=[histr])
            else:
                E(DVE, lambda: nc.vector.tensor_copy(out=hist[:, :, :, 0], in_=hist[:, :, :, NK8]), R=[histr], W=[histr])
            if full:
                for b in range(8):
                    wc32, wc64, kb, cr = load_s5cc(b)
                    p_, p_r = nps()
                    pv = p_[:, 0:TT].rearrange("p (k j) -> p k j", j=8)
                    uv = uT[:, b, :].rearrange("p (k j) -> p k j", j=8)
                    for tau in range(8):
                        mm(pv[:, :, tau:8], kb[:, tau, :], uv[:, :, 0:8 - tau], tau == 0, False, R=[cr, uTr[b]], W=[p_r])
                    for r in range(4):
                        if r < 2:
                            pvr = p_[32 * r:32 * r + 32, 0:TT].rearrange("p (k j) -> p k j", j=8)
                        else:
                            pvr = p_[64:128, 0:TT].rearrange("p (k j) -> p k j", j=8)
                        for j in range(8):
                            for ri in range(2):
                                last = (r == 3 and j == 7 and ri == 1)
                                lhs = wc32[:, r, j, ri, :] if r < 2 else wc64[:, r - 2, j, ri, :]
                                mm(pvr[:, :, j], lhs, histb[:, ri, 4 * b + r, :], False, last, R=[cr, histbr], W=[p_r])
                    y_, y_r = ntmp()
                    stt(y_[:, 0:TT], uT[:, b, :], DqT[:, b:b + 1], p_[:, 0:TT], ALU.mult, ALU.add, R=[uTr[b], p_r] + C, W=[y_r])
                    act(zT[:, b, :], y_[:, 0:TT], AF.Gelu, R=[y_r], W=[zTr])
                for g4 in range(4):
                    wsb, wr = wload(w_glu[:, g4 * 256:(g4 + 1) * 256], 1024, 256)
                    for j in range(2):
                        m = g4 * 2 + j
                        p_, p_r = fm_group(wsb, wr, 8, j * 128, zT, [zTr])
                        sg, sgr = ntmp()
                        act(sg[:, 0:TT], p_[:, 0:TT], AF.Sigmoid, R=[p_r] + C, W=[sgr], bias=bgluT[:, m:m + 1])
                        tt(s5o[:, m, :], zT[:, m, :], sg[:, 0:TT], ALU.mult, R=[zTr, sgr], W=[s5or])

            if not full:
                if has_next:
                    norm_b(gmixT)
                continue
            for m2 in range(8):
                w1, w1r = wload(w_ps5[:, m2 * 256:(m2 + 1) * 256], 1024, 256)
                w2, w2r = wload(w_in[:, 5120 + m2 * 256:5120 + (m2 + 1) * 256], D, 256)
                w3, w3r = wload(w_phg[:, m2 * 256:(m2 + 1) * 256], 1024, 256)
                w4, w4r = wload(w_in[:, 7168 + m2 * 256:7168 + (m2 + 1) * 256], D, 256)
                for j in range(2):
                    m = m2 * 2 + j
                    p1, p1r = fm_group(w1, w1r, 8, j * 128, s5o, [s5or])
                    p2, p2r = fm_group(w2, w2r, 16, j * 128, hT, hTr)
                    g1, g1r = ntmp()
                    act(g1[:, 0:TT], p2[:, 0:TT], AF.Sigmoid, R=[p2r], W=[g1r])
                    tt(g1[:, 0:TT], g1[:, 0:TT], p1[:, 0:TT], ALU.mult, R=[g1r, p1r], W=[g1r])
                    p3, p3r = fm_group(w3, w3r, 8, j * 128, ohT, [ohTr])
                    p4, p4r = fm_group(w4, w4r, 16, j * 128, hT, hTr)
                    g2, g2r = ntmp()
                    act(g2[:, 0:TT], p4[:, 0:TT], AF.Sigmoid, R=[p4r], W=[g2r])
                    tt(g2[:, 0:TT], g2[:, 0:TT], p3[:, 0:TT], ALU.mult, R=[g2r, p3r], W=[g2r])
                    tt(mT[:, m, :], g1[:, 0:TT], g2[:, 0:TT], ALU.add, R=[g1r, g2r], W=mTr)
            if pend_epi is not None:
                epilogue(*pend_epi)
                pend_epi = None
            for n in range(8):
                wsb, wr = wload(w_out[:, n * 256:(n + 1) * 256], D, 256)
                for sub in range(NSUB):
                    p_, p_r = nps()
                    for kk in range(16):
                        mm(p_[:, 0:256], mT[:, kk, sub * 128:(sub + 1) * 128], wsb[:, kk, :], kk == 0, kk == 15, R=[wr] + mTr, W=[p_r])
                    tt(xres[:, xb, sub, n * 256:(n + 1) * 256], xres[:, xb, sub, n * 256:(n + 1) * 256], p_[:, 0:256], ALU.add, R=[p_r, xr[xb][sub]], W=[xr[xb][sub]])
            xload(t + 1)
            rmsnorm_to_hT(gffnT, xb)
            for c in range(NFC):
                if c % 2 == 0:
                    wg, wgr = wload(w_up[:, (c // 2) * 256:(c // 2 + 1) * 256], D, 256)
                    wv, wvr = wload(w_up[:, DFF + (c // 2) * 256:DFF + (c // 2 + 1) * 256], D, 256)
                pg, pgr = fm_group(wg, wgr, 16, (c % 2) * 128, hT, hTr)
                pv_, pvr_ = fm_group(wv, wvr, 16, (c % 2) * 128, hT, hTr)
                if not main:
                    act(chist[:, c, 0, :], pg[:, TT - 2:TT], AF.Copy, R=[pgr], W=[chr_[c]])
                    act(chist[:, c, 1, :], pv_[:, TT - 2:TT], AF.Copy, R=[pvr_], W=[chr_[c]])
                    continue
                ug, ugr = ntmp()
                uv_, uvr = ntmp()
                act(ug[:, 2:TT + 2], pg[:, 0:TT], AF.Copy, R=[pgr], W=[ugr])
                act(uv_[:, 2:TT + 2], pv_[:, 0:TT], AF.Copy, R=[pvr_], W=[uvr])
                act(ug[:, 0:2], chist[:, c, 0, :], AF.Copy, R=[chr_[c]], W=[ugr])
                act(uv_[:, 0:2], chist[:, c, 1, :], AF.Copy, R=[chr_[c]], W=[uvr])
                act(chist[:, c, 0, :], ug[:, TT:TT + 2], AF.Copy, R=[ugr], W=[chr_[c]])
                act(chist[:, c, 1, :], uv_[:, TT:TT + 2], AF.Copy, R=[uvr], W=[chr_[c]])
                res_ = []
                for vi, (u_, u_r) in enumerate(((ug, ugr), (uv_, uvr))):
                    a_, a_r = ntmp()
                    ts(a_[:, 0:TT], u_[:, 2:TT + 2], cw[:, c, vi, 2:3], cb[:, c, vi:vi + 1], ALU.mult, ALU.add, R=[u_r] + C, W=[a_r])
                    stt(a_[:, 0:TT], u_[:, 1:TT + 1], cw[:, c, vi, 1:2], a_[:, 0:TT], ALU.mult, ALU.add, R=[u_r, a_r] + C, W=[a_r])
                    stt(a_[:, 0:TT], u_[:, 0:TT], cw[:, c, vi, 0:1], a_[:, 0:TT], ALU.mult, ALU.add, R=[u_r, a_r] + C, W=[a_r])
                    res_.append((a_, a_r))
                (ag, agr), (av, avr) = res_
                act(ug[:, 0:TT], ag[:, 0:TT], AF.Silu, R=[agr], W=[ugr])
                tt(actT[:, c, :], ug[:, 0:TT], av[:, 0:TT], ALU.mult, R=[ugr, avr], W=actTR)
            if not main:
                if has_next:
                    rmsnorm_to_hT(gmixT, (t + 1) % 2)
                continue
            for n in range(8):
                if n == 2 and has_next:
                    norm_a((t + 1) % 2)
                if n == 5 and has_next:
                    norm_b(gmixT)
                pss = [nps() for _ in range(NSUB)]
                kg = 0
                for grp in (16, 16, 12):
                    wsb, wr = wload(w_down[kg * 128:(kg + grp) * 128, n * 256:(n + 1) * 256], grp * 128, 256)
                    for sub in range(NSUB):
                        for kk in range(grp):
                            kgl = kg + kk
                            mm(pss[sub][0][:, 0:256], actT[:, kgl, sub * 128:(sub + 1) * 128], wsb[:, kk, :], kgl == 0, kgl == NFC - 1,
                               R=[wr] + actTR, W=[pss[sub][1]])
                    kg += grp
                for sub in range(NSUB):
                    tt(xres[:, xb, sub, n * 256:(n + 1) * 256], xres[:, xb, sub, n * 256:(n + 1) * 256], pss[sub][0][:, 0:256], ALU.add,
                       R=[pss[sub][1], xr[xb][sub]], W=[xr[xb][sub]])
            pend_epi = (t, xb)
        if pend_epi is not None:
            epilogue(*pend_epi)
        if osem.cnt:
            nc.sync.wait_ge(osem.h, osem.cnt)
        k.barrier([PE, ACT, DVE, SP, POOL])
    return nc


_NC = None


def kernel(**inputs):
    global _NC
    x = np.ascontiguousarray(inputs["x"], dtype=np.float32)
    B, L, _ = x.shape
    ident = np.eye(128, dtype=np.float32)
    idx = np.arange(128)
    maskbd = ((idx[:, None] // 64 == idx[None, :] // 64) & (idx[:, None] <= idx[None, :])).astype(np.float32)
    rmask = np.ones((128, TT), np.float32)
    rmask[:, ::64] = 0.0
    me2 = np.stack([((idx // 16) % 2 == e) for e in range(2)], axis=1).astype(np.float32)
    me1 = np.stack([(idx // 64 == e) for e in range(2)], axis=1).astype(np.float32)
    mkc = np.stack([(idx // 32 == r_) for r_ in range(4)], axis=1).astype(np.float32)
    shared = {
        "ln_mix_g": inputs["ln_mix_g"][0], "w_in": inputs["w_in"][0],
        "s5_a_re": inputs["s5_a_re"][0], "s5_a_im": inputs["s5_a_im"][0], "s5_log_dt": inputs["s5_log_dt"][0],
        "s5_b_re": inputs["s5_b_re"][0], "s5_b_im": inputs["s5_b_im"][0], "s5_c_re": inputs["s5_c_re"][0],
        "s5_c_im": inputs["s5_c_im"][0], "s5_d": inputs["s5_d"][0], "s5_w_glu": inputs["s5_w_glu"][0],
        "s5_b_glu": inputs["s5_b_glu"][0], "w_proj_s5": inputs["w_proj_s5"][0], "hgrn_lb_logits": inputs["hgrn_lb_logits"],
        "hgrn_norm_g": inputs["hgrn_norm_g"][0], "w_proj_hgrn": inputs["w_proj_hgrn"][0], "w_out": inputs["w_out"][0],
        "ln_ffn_g": inputs["ln_ffn_g"][0], "w_up": inputs["w_up"][0], "conv_w": inputs["conv_w"][0],
        "conv_b": inputs["conv_b"][0], "w_down": inputs["w_down"][0], "ln_final_g": inputs["ln_final_g"],
        "c_mk": mkc, "c_ident": ident, "c_identf": ident, "c_maskbd": maskbd, "c_rmask": rmask, "c_me2": me2, "c_me1": me1,
    }
    shared = {k_: np.ascontiguousarray(v, dtype=np.float32) for k_, v in shared.items()}
    in_maps = []
    for c in range(8):
        b, h = c // 2, c % 2
        xi = np.zeros((2 * HALF, D), np.float32)
        if h == 1:
            xi[:HALF] = x[b, :HALF]
        xi[HALF:] = x[b, h * HALF:(h + 1) * HALF]
        m = dict(shared)
        m["xin"] = xi
        in_maps.append(m)
    if _NC is None:
        _NC = build_nc()
    res = run_bass_kernel_spmd(_NC, in_maps, core_ids=list(range(8)))
    outp = np.empty((B, L, D), np.float32)
    for c in range(8):
        b, h = c // 2, c % 2
        outp[b, h * HALF:(h + 1) * HALF] = res.results[c]["out"]
    return outp
```
